# Optimizing a Trainium2 kernel written in Bass

```python
import jax, jax.numpy as jnp
from jax import lax
import numpy as np

D_MODEL = 1024
BATCH = 16
SEQ = 2048
DEPTH = 1

D_POOL = D_MODEL
POOL_GROUPS = 4
POOL_GROUP_DIM = D_POOL // POOL_GROUPS
POOL_WINDOWS = (2, 4, 8, 16)
D_MLSTM = D_MODEL
MLSTM_HEADS = 4
HEAD_DIM = D_MLSTM // MLSTM_HEADS
N_DIRS = 2
CONV_WIDTH = 5
CHUNK = 128
N_BRANCHES = 2
EPS = 1e-6
NEG_INIT = -1e30
SPLIT_SIZES = (D_POOL, D_POOL, D_MLSTM, D_MLSTM, D_MLSTM, D_MLSTM, D_MLSTM,
               2 * N_DIRS * MLSTM_HEADS, D_MODEL, D_MODEL)
D_IN = 2 * D_POOL + 5 * D_MLSTM + 2 * N_DIRS * MLSTM_HEADS + N_BRANCHES * D_MODEL

kernel_name = "hybrid_pool_mlstm_gated_block"


def rms_norm(x, g):
    xf = x.astype(jnp.float32)
    y = xf * lax.rsqrt(jnp.mean(xf * xf, axis=-1, keepdims=True) + EPS)
    return (y * g.astype(jnp.float32)).astype(x.dtype)


def multiscale_pool(u):
    B, S, _ = u.shape
    uf = u.astype(jnp.float32)
    cs = jnp.concatenate([jnp.zeros((B, 1, D_POOL), jnp.float32), jnp.cumsum(uf, axis=1)], axis=1)
    t = jnp.arange(S)
    outs = []
    for g, w in enumerate(POOL_WINDOWS):
        lo = jnp.clip(t - w // 2, 0, S - 1)
        hi = jnp.clip(t + (w - 1 - w // 2), 0, S - 1)
        sl = slice(g * POOL_GROUP_DIM, (g + 1) * POOL_GROUP_DIM)
        csg = cs[:, :, sl]
        window_sum = jnp.take(csg, hi + 1, axis=1) - jnp.take(csg, lo, axis=1)
        count = (hi - lo + 1).astype(jnp.float32)[None, :, None]
        outs.append(window_sum / count - uf[:, :, sl])
    return jnp.concatenate(outs, axis=-1)


def centred_depthwise_conv(x, w, b):
    C = x.shape[-1]
    y = lax.conv_general_dilated(
        x, w[:, None, :].astype(x.dtype), window_strides=(1,),
        padding=[(CONV_WIDTH // 2, CONV_WIDTH // 2)],
        dimension_numbers=('NWC', 'WIO', 'NWC'), feature_group_count=C)
    return y + b.astype(x.dtype)


def mlstm_direction(q, k, v, i_pre, f_pre):
    B, H, S, Dh = q.shape
    nc = S // CHUNK
    qc = q.reshape(B, H, nc, CHUNK, Dh) * (Dh ** -0.5)
    kc = k.reshape(B, H, nc, CHUNK, Dh)
    vc = v.reshape(B, H, nc, CHUNK, Dh)
    log_f = jax.nn.log_sigmoid(f_pre).reshape(B, H, nc, CHUNK)
    log_i = i_pre.reshape(B, H, nc, CHUNK)
    b = jnp.cumsum(log_f, axis=-1)
    b_last = b[..., -1]

    w_end = b_last[..., None] - b + log_i
    m_chunk = jnp.max(w_end, axis=-1)
    kw = kc * jnp.exp(w_end - m_chunk[..., None])[..., None]
    c_chunk = jnp.einsum('bhcld,bhcle->bhcde', kw, vc)
    n_chunk = jnp.sum(kw, axis=3)

    def step(carry, xs):
        c_st, n_st, m_st = carry
        c_add, n_add, m_add, bl, q_blk = xs
        num_inter = jnp.einsum('bhld,bhde->bhle', q_blk, c_st)
        den_inter = jnp.einsum('bhld,bhd->bhl', q_blk, n_st)
        m_new = jnp.maximum(bl + m_st, m_add)
        keep = jnp.exp(bl + m_st - m_new)
        add = jnp.exp(m_add - m_new)
        c_new = keep[..., None, None] * c_st + add[..., None, None] * c_add
        n_new = keep[..., None] * n_st + add[..., None] * n_add
        return (c_new, n_new, m_new), (num_inter, den_inter, m_st)

    init = (jnp.zeros((B, H, Dh, Dh), jnp.float32),
            jnp.zeros((B, H, Dh), jnp.float32),
            jnp.full((B, H), NEG_INIT, jnp.float32))
    xs = (jnp.moveaxis(c_chunk, 2, 0), jnp.moveaxis(n_chunk, 2, 0), jnp.moveaxis(m_chunk, 2, 0),
          jnp.moveaxis(b_last, 2, 0), jnp.moveaxis(qc, 2, 0))
    _, (num_inter, den_inter, m_prev) = lax.scan(step, init, xs)
    num_inter = jnp.moveaxis(num_inter, 0, 2)
    den_inter = jnp.moveaxis(den_inter, 0, 2)
    m_prev = jnp.moveaxis(m_prev, 0, 2)

    mask = jnp.tril(jnp.ones((CHUNK, CHUNK), dtype=bool))
    log_d = jnp.where(mask, b[..., :, None] - b[..., None, :] + log_i[..., None, :], -jnp.inf)
    log_inter = b + m_prev[..., None]
    m_t = jnp.maximum(jnp.max(log_d, axis=-1), log_inter)
    d = jnp.exp(log_d - m_t[..., None])
    inter_scale = jnp.exp(log_inter - m_t)
    scores = jnp.einsum('bhcld,bhcsd->bhcls', qc, kc) * d
    num = jnp.einsum('bhcls,bhcsd->bhcld', scores, vc) + inter_scale[..., None] * num_inter
    den = jnp.sum(scores, axis=-1) + inter_scale * den_inter
    h = num / jnp.maximum(jnp.abs(den), jnp.exp(-m_t))[..., None]
    return h.reshape(B, H, S, Dh)


def hybrid_layer(x, norm_pre_g, w_in, conv_qk_w, conv_qk_b, b_igate, b_fgate, pool_mix_w,
                 pool_scale, mlstm_norm_g, w_proj_pool, w_proj_mlstm, b_merge, w_out, norm_post_g):
    B, S, _ = x.shape
    h = rms_norm(x, norm_pre_g)
    proj = jnp.einsum('bsd,de->bse', h, w_in.astype(h.dtype))
    split_points = [int(p) for p in np.cumsum(SPLIT_SIZES)[:-1]]
    u_a, z_a, q, k, v, o, z_b, gates, g_a, g_b = jnp.split(proj, split_points, axis=-1)

    pooled = multiscale_pool(u_a)
    mixed = jnp.einsum('bsgc,gcd->bsgd', pooled.reshape(B, S, POOL_GROUPS, POOL_GROUP_DIM),
                       pool_mix_w.astype(jnp.float32)).reshape(B, S, D_POOL)
    y_a = (mixed * pool_scale.astype(jnp.float32)).astype(x.dtype) * jax.nn.silu(z_a)

    qk = jax.nn.silu(centred_depthwise_conv(jnp.concatenate([q, k], axis=-1), conv_qk_w, conv_qk_b))
    q, k = jnp.split(qk, 2, axis=-1)

    def to_heads(t):
        return t.astype(jnp.float32).reshape(B, S, MLSTM_HEADS, HEAD_DIM).transpose(0, 2, 1, 3)

    qh, kh, vh = to_heads(q), to_heads(k), to_heads(v)
    gates = gates.astype(jnp.float32).reshape(B, S, N_DIRS, 2, MLSTM_HEADS)
    i_pre = (gates[:, :, :, 0, :] + b_igate.astype(jnp.float32)).transpose(2, 0, 3, 1)
    f_pre = (gates[:, :, :, 1, :] + b_fgate.astype(jnp.float32)).transpose(2, 0, 3, 1)
    h_fwd = mlstm_direction(qh, kh, vh, i_pre[0], f_pre[0])
    flip = lambda t: jnp.flip(t, axis=2)
    h_bwd = flip(mlstm_direction(flip(qh), flip(kh), flip(vh), flip(i_pre[1]), flip(f_pre[1])))
    hb = h_fwd + h_bwd
    hb = hb * lax.rsqrt(jnp.mean(hb * hb, axis=-1, keepdims=True) + EPS)
    hb = hb.transpose(0, 2, 1, 3).reshape(B, S, D_MLSTM) * mlstm_norm_g.astype(jnp.float32)
    y_b = hb.astype(x.dtype) * jax.nn.sigmoid(o) * jax.nn.silu(z_b)

    p_a = jnp.einsum('bsc,cd->bsd', y_a, w_proj_pool.astype(y_a.dtype))
    p_b = jnp.einsum('bsc,cd->bsd', y_b, w_proj_mlstm.astype(y_b.dtype))
    merged = jax.nn.sigmoid(g_a + b_merge[0]) * p_a + jax.nn.sigmoid(g_b + b_merge[1]) * p_b
    out = jnp.einsum('bsd,de->bse', merged, w_out.astype(merged.dtype))
    return x + rms_norm(out, norm_post_g)


def setup_inputs(seed: int = 0) -> dict:
    key = jax.random.key(seed)
    ks = jax.random.split(key, 16)
    n = jax.random.normal
    f32 = jnp.float32
    x = n(ks[0], (BATCH, SEQ, D_MODEL), f32)
    norm_pre_g = 1.0 + 0.1 * n(ks[1], (DEPTH, D_MODEL), f32)
    w_in = n(ks[2], (DEPTH, D_MODEL, D_IN), f32) * D_MODEL ** -0.5
    conv_qk_w = n(ks[3], (DEPTH, CONV_WIDTH, 2 * D_MLSTM), f32) * CONV_WIDTH ** -0.5
    conv_qk_b = 0.02 * n(ks[4], (DEPTH, 2 * D_MLSTM), f32)
    b_igate = 0.1 * n(ks[5], (DEPTH, N_DIRS, MLSTM_HEADS), f32)
    b_fgate = jnp.linspace(3.0, 6.0, MLSTM_HEADS, dtype=f32)[None, None, :] + 0.1 * n(ks[6], (DEPTH, N_DIRS, MLSTM_HEADS), f32)
    pool_mix_w = n(ks[7], (DEPTH, POOL_GROUPS, POOL_GROUP_DIM, POOL_GROUP_DIM), f32) * POOL_GROUP_DIM ** -0.5
    pool_scale = 1.0 + 0.1 * n(ks[8], (DEPTH, D_POOL), f32)
    mlstm_norm_g = 1.0 + 0.1 * n(ks[9], (DEPTH, D_MLSTM), f32)
    w_proj_pool = n(ks[10], (DEPTH, D_POOL, D_MODEL), f32) * D_POOL ** -0.5
    w_proj_mlstm = n(ks[11], (DEPTH, D_MLSTM, D_MODEL), f32) * D_MLSTM ** -0.5
    b_merge = 0.1 * n(ks[12], (DEPTH, N_BRANCHES, D_MODEL), f32)
    w_out = n(ks[13], (DEPTH, D_MODEL, D_MODEL), f32) * D_MODEL ** -0.5
    norm_post_g = 1.0 + 0.1 * n(ks[14], (DEPTH, D_MODEL), f32)
    return {"x": x, "norm_pre_g": norm_pre_g, "w_in": w_in, "conv_qk_w": conv_qk_w,
            "conv_qk_b": conv_qk_b, "b_igate": b_igate, "b_fgate": b_fgate,
            "pool_mix_w": pool_mix_w, "pool_scale": pool_scale, "mlstm_norm_g": mlstm_norm_g,
            "w_proj_pool": w_proj_pool, "w_proj_mlstm": w_proj_mlstm, "b_merge": b_merge,
            "w_out": w_out, "norm_post_g": norm_post_g}


def reference(x, norm_pre_g, w_in, conv_qk_w, conv_qk_b, b_igate, b_fgate, pool_mix_w, pool_scale,
              mlstm_norm_g, w_proj_pool, w_proj_mlstm, b_merge, w_out, norm_post_g):
    for l in range(DEPTH):
        x = hybrid_layer(x, norm_pre_g[l], w_in[l], conv_qk_w[l], conv_qk_b[l], b_igate[l], b_fgate[l],
                         pool_mix_w[l], pool_scale[l], mlstm_norm_g[l], w_proj_pool[l], w_proj_mlstm[l],
                         b_merge[l], w_out[l], norm_post_g[l])
    return x
```

```python
import numpy as np
import ml_dtypes
import concourse.bass as bass
import concourse.mybir as mybir
from concourse.bass_utils import run_bass_kernel_spmd

F32 = mybir.dt.float32
BF16 = mybir.dt.bfloat16
I32 = mybir.dt.int32
AF = mybir.ActivationFunctionType
ALU = mybir.AluOpType

D = 1024
NH = 4
DH = 256
EPS = 1e-6
BLK = 256


class V:
    def __init__(s, ap, space, off, shape, strides, esz):
        s.ap, s.space, s.off, s.shape, s.strides, s.esz = ap, space, off, list(shape), list(strides), esz
        s._b = None

    def __getitem__(s, idx):
        if not isinstance(idx, tuple):
            idx = (idx,)
        idx = list(idx) + [slice(None)] * (len(s.shape) - len(idx))
        off = s.off
        shape, strides = [], []
        for d, (i, n) in enumerate(zip(idx, s.shape)):
            if isinstance(i, int):
                if d == 0:
                    shape.append(1); strides.append(0)
                    idx[0] = slice(i, i + 1)
                else:
                    off += i * s.strides[d]
            else:
                a = 0 if i.start is None else i.start
                b = n if i.stop is None else i.stop
                assert 0 <= a < b <= n, (a, b, n)
                if d > 0:
                    off += a * s.strides[d]
                shape.append(b - a); strides.append(s.strides[d])
        return V(s.ap[tuple(idx)], s.space, off, shape, strides, s.esz)

    def blocks(s):
        if s._b is None and s.space == "ps":
            s._b = {("ps", s.off // 2048)}
        if s._b is None:
            res = set()
            fs, st = s.shape[1:], s.strides[1:]
            if not fs:
                fs, st = [1], [s.esz]
            run = fs[-1] * st[-1]
            outer = fs[:-1]
            idxs = [0] * len(outer)
            while True:
                o = s.off + sum(i * t for i, t in zip(idxs, st[:-1]))
                for b in range(o // BLK, (o + run - 1) // BLK + 1):
                    res.add((s.space, b))
                k = len(outer) - 1
                while k >= 0:
                    idxs[k] += 1
                    if idxs[k] < outer[k]:
                        break
                    idxs[k] = 0
                    k -= 1
                if k < 0:
                    break
            s._b = res
        return s._b


def _keys(lst):
    out = set()
    for x in lst:
        if isinstance(x, V):
            out |= x.blocks()
        else:
            out.add(x)
    return out


class Op:
    __slots__ = ("eng", "emit", "deps", "sig", "sem", "val", "dma")

    def __init__(s, eng, emit, dma):
        s.eng, s.emit, s.dma = eng, emit, dma
        s.deps = set(); s.sig = dma; s.sem = None; s.val = 0


class Sched:
    ENGS = ("pe", "act", "dve", "pool", "sp")

    def __init__(s):
        s.ops = {e: [] for e in s.ENGS}
        s.lw, s.rd = {}, {}

    def _dep(s, op, d, kind):
        if d is op:
            return
        if d.eng == op.eng and not d.dma and not op.dma:
            if op.eng == "pe":
                return
        op.deps.add(d); d.sig = True

    def add(s, eng, emit, reads=(), writes=(), dma=False):
        op = Op(eng, emit, dma)
        rk, wk = _keys(reads), _keys(writes)
        for k in rk:
            w = s.lw.get(k)
            if w is not None:
                s._dep(op, w, "raw")
        for k in wk:
            w = s.lw.get(k)
            if w is not None:
                s._dep(op, w, "waw")
            for r in s.rd.get(k, ()):
                s._dep(op, r, "war")
        for k in rk:
            s.rd.setdefault(k, []).append(op)
        for k in wk:
            s.lw[k] = op; s.rd[k] = []
        s.ops[eng].append(op)
        return op

    def finalize(s, sems, dsems):
        for e in s.ENGS:
            cnt = 0
            pool = dsems.get(e, [])
            use = [0] * len(pool)
            last = [None] * len(pool)
            nd = 0
            for op in s.ops[e]:
                if op.dma:
                    j = nd % len(pool); nd += 1
                    use[j] += 1
                    op.sem, op.val = pool[j], 16 * use[j]
                    if last[j] is not None:
                        op.deps.add(last[j])
                    last[j] = op
                elif op.sig:
                    cnt += 1
                    op.sem, op.val = sems[e], cnt

    def emit(s, e, eng):
        waited = {}
        for op in s.ops[e]:
            need = {}
            for d in op.deps:
                k = d.sem.num
                if need.get(k, (None, 0))[1] < d.val:
                    need[k] = (d.sem, d.val)
            for k, (sem, val) in need.items():
                if waited.get(k, 0) < val:
                    eng.wait_ge(sem, val); waited[k] = val
            ins = op.emit(eng)
            if op.sig and ins is not None:
                ins.then_inc(op.sem, 16 if op.dma else 1)


POOL_WINDOWS = (2, 4, 8, 16)


def _pool_mats(S):
    out = np.zeros((4, 5, 128, 128), np.float32)
    for g, w in enumerate(POOL_WINDOWS):
        def full(t_glob0, tp_glob0, S=S):
            M = np.zeros((128, 128), np.float32)
            for t in range(128):
                tg = t_glob0 + t
                lo = min(max(tg - w // 2, 0), S - 1)
                hi = min(max(tg + (w - 1 - w // 2), 0), S - 1)
                cnt = hi - lo + 1
                for tp in range(128):
                    tpg = tp_glob0 + tp
                    v = 0.0
                    if lo <= tpg <= hi:
                        v += 1.0 / cnt
                    if tpg == tg:
                        v -= 1.0
                    M[tp, t] = v
            return M
        mid = 128 * 8
        out[g, 0] = full(mid, mid - 128, 1 << 20)
        out[g, 1] = full(mid, mid + 128, 1 << 20)
        out[g, 2] = full(mid, mid, 1 << 20)
        out[g, 3] = full(0, 0)
        out[g, 4] = full(S - 128, S - 128)
    return out


def _consts(S):
    ident = np.eye(128, dtype=np.float32)
    s = np.arange(128)
    maskF = (s[:, None] <= s[None, :]).astype(np.float32)
    maskB = (s[:, None] >= s[None, :]).astype(np.float32)
    ones = np.ones((128, 128), np.float32)
    pm = _pool_mats(S).reshape(20, 128, 128)
    allm = np.concatenate([ident[None], maskF[None], maskB[None], ones[None], pm], 0)
    return np.ascontiguousarray(allm.transpose(1, 0, 2).reshape(128, 24 * 128))


P_GPRE, P_PSC, P_MG, P_CW, P_BM, P_GB, P_GPOST = 0, 8, 16, 24, 104, 120, 136
PX = 136 + 1024


def _params(inp):
    p = np.zeros((128, PX), np.float32)
    def pc(v):
        return np.asarray(v, np.float32).reshape(8, 128).T
    p[:, P_GPRE:P_GPRE + 8] = pc(inp["norm_pre_g"][0])
    p[:, P_PSC:P_PSC + 8] = pc(inp["pool_scale"][0])
    p[:, P_MG:P_MG + 8] = pc(inp["mlstm_norm_g"][0])
    cw = np.asarray(inp["conv_qk_w"][0], np.float32)
    p[:, P_CW:P_CW + 80] = cw.reshape(5, 16, 128).transpose(2, 1, 0).reshape(128, 80)
    bm = np.asarray(inp["b_merge"][0], np.float32)
    p[:, P_BM:P_BM + 16] = bm.reshape(2, 8, 128).transpose(2, 0, 1).reshape(128, 16)
    gb = np.zeros((2, 2, 4), np.float32)
    gb[:, 0, :] = np.asarray(inp["b_igate"][0], np.float32)
    gb[:, 1, :] = np.asarray(inp["b_fgate"][0], np.float32)
    p[:, P_GB:P_GB + 16] = np.broadcast_to(gb.reshape(1, 16), (128, 16))
    p[:, P_GPOST:P_GPOST + 1024] = np.broadcast_to(np.asarray(inp["norm_post_g"][0], np.float32)[None, :], (128, 1024))
    return p


B_U, B_HEAD, B_GATE, B_ZA, B_P3, B_OUT, B_MIX, NB = 0, 8, 48, 49, 57, 89, 97, 99
K_Q, K_K, K_V, K_O, K_ZB = 0, 1, 2, 3, 4
K_PP, K_GA, K_PM, K_GBR = 0, 1, 2, 3


def build(S, NSEQ):
    NT = S // 128
    NG = S // 512
    nc = bass.Bass("TRN2", target_bir_lowering=False)
    dt = lambda n, sh, ty, k: nc.dram_tensor(n, sh, ty, kind=k).ap()
    x_d = dt("x", [NSEQ, S, D], F32, "ExternalInput")
    win_d = dt("w_in", [D, 9232], F32, "ExternalInput")
    wpp_d = dt("w_pp", [D, D], F32, "ExternalInput")
    wpm_d = dt("w_pm", [D, D], F32, "ExternalInput")
    wout_d = dt("w_out", [D, D], F32, "ExternalInput")
    wmix_d = dt("w_mix", [D, 256], F32, "ExternalInput")
    cst_d = dt("consts", [128, 24 * 128], F32, "ExternalInput")
    par_d = dt("params", [128, PX], F32, "ExternalInput")
    cvb_d = dt("convb", [1, 2048], F32, "ExternalInput")
    y_d = dt("y", [NSEQ, S, D], F32, "ExternalOutput")
    wsc = dt("wsc", [NB, 128, 1024], BF16, "Internal")

    sc = Sched()
    AW = 212000 // 4 // 64 * 64

    import contextlib
    with contextlib.ExitStack() as es:
        arena = es.enter_context(nc.sbuf_tensor("arena", [128, AW], F32))
        banks = [es.enter_context(nc.psum_tensor("ps%d" % i, [128, 512], F32)) for i in range(8)]
        csems = {e: es.enter_context(nc.semaphore("s_" + e)) for e in ("pe", "act", "dve", "pool")}
        dsems = {"sp": [es.enter_context(nc.semaphore("d_sp%d" % i)) for i in range(12)],
                 "act": [es.enter_context(nc.semaphore("d_act%d" % i)) for i in range(6)],
                 "pool": [es.enter_context(nc.semaphore("d_pool%d" % i)) for i in range(6)]}
        block = es.enter_context(nc.Block())

        class Arena:
            def __init__(s): s.top = 0
            def alloc(s, shape, ty=F32, at=None):
                esz = 4 if ty in (F32, I32) else 2
                n = int(np.prod(shape))
                words = (n * esz + 3) // 4
                words = (words + 63) // 64 * 64
                if at is None:
                    at = s.top; s.top += words
                assert at + words <= AW, ("arena overflow", at, words, AW)
                ap = arena[:, at:at + words]
                if ty != F32:
                    ap = ap.bitcast(ty)
                ap = ap[:, 0:n]
                if len(shape) == 2:
                    names = "p (a b) -> p a b"; ap = ap.rearrange(names, a=shape[0])
                elif len(shape) == 3:
                    ap = ap.rearrange("p (a b c) -> p a b c", a=shape[0], b=shape[1])
                elif len(shape) == 4:
                    ap = ap.rearrange("p (a b c d) -> p a b c d", a=shape[0], b=shape[1], c=shape[2])
                strides = []
                acc = esz
                for d_ in reversed(shape):
                    strides.insert(0, acc); acc *= d_
                return V(ap, "sb", at * 4, [128] + list(shape), [0] + strides, esz)
        A = Arena()

        def psv(b, ty=F32):
            if ty == F32:
                return V(banks[b][:, :], "ps", b * 2048, [128, 512], [0, 4], 4)
            return V(banks[b][:, :].bitcast(BF16), "ps", b * 2048, [128, 1024], [0, 2], 2)

        def dma(q, out, in_, reads, writes):
            sc.add(q, lambda e: e.dma_start(out=out, in_=in_), reads=reads, writes=writes, dma=True)

        def act(out, in_, func, scale=1.0, bias=0.0, accum=None, extra_r=()):
            r = [in_] + list(extra_r)
            sa = scale.ap if isinstance(scale, V) else scale
            ba = bias.ap if isinstance(bias, V) else bias
            if isinstance(scale, V): r.append(scale)
            if isinstance(bias, V): r.append(bias)
            w = [out] + ([accum] if accum is not None else [])
            kw = {}
            if accum is not None:
                kw["accum_out"] = accum.ap
            sc.add("act", lambda e: e.activation(out.ap, in_.ap, func, bias=ba, scale=sa, **kw), reads=r, writes=w)

        def tt(eng, out, a, b, op):
            sc.add(eng, lambda e: e.tensor_tensor(out.ap, a.ap, b.ap, op), reads=[a, b], writes=[out])

        def ts(eng, out, a, s1, s2, op0, op1=None):
            r = [a] + [x for x in (s1, s2) if isinstance(x, V)]
            s1a = s1.ap if isinstance(s1, V) else s1
            s2a = s2.ap if isinstance(s2, V) else s2
            if op1 is None:
                sc.add(eng, lambda e: e.tensor_scalar(out.ap, a.ap, s1a, None, op0), reads=r, writes=[out])
            else:
                sc.add(eng, lambda e: e.tensor_scalar(out.ap, a.ap, s1a, s2a, op0, op1), reads=r, writes=[out])

        def stt(eng, out, a, s, b, op0, op1, b_ap=None, b_reads=None):
            r = [a] + ([s] if isinstance(s, V) else []) + ([b] if b_reads is None else list(b_reads))
            sa = s.ap if isinstance(s, V) else s
            bap = b.ap if b_ap is None else b_ap
            sc.add(eng, lambda e: e.scalar_tensor_tensor(out.ap, a.ap, sa, bap, op0, op1), reads=r, writes=[out])

        def cp(eng, out, in_):
            if eng == "act":
                sc.add("act", lambda e: e.copy(out.ap, in_.ap), reads=[in_], writes=[out])
            else:
                sc.add(eng, lambda e: e.tensor_copy(out.ap, in_.ap), reads=[in_], writes=[out])

        def memset(eng, out, val):
            sc.add(eng, lambda e: e.memset(out.ap, val), writes=[out])

        def mm(out, pairs):
            r = []
            for l, rh in pairs:
                r += [l, rh]
            n = len(pairs)
            def em(e):
                ins = None
                for i, (l, rh) in enumerate(pairs):
                    ins = e.matmul(out.ap, l.ap, rh.ap, start=(i == 0), stop=(i == n - 1))
                return ins
            sc.add("pe", em, reads=r, writes=[out])

        def tr(out, in_, ident):
            sc.add("pe", lambda e: e.transpose(out.ap, in_.ap, ident.ap), reads=[in_, ident], writes=[out])

        def rsqrt(eng, y, x, t1):
            sc.add(eng, lambda e: e.tensor_scalar(y.ap.bitcast(I32), x.ap.bitcast(I32), -0.5, float(0x5f3759df),
                                                  ALU.mult, ALU.add), reads=[x], writes=[y])
            for _ in range(3):
                stt(eng, t1, y, -0.5, y, ALU.mult, ALU.mult)
                tt(eng, t1, t1, x, ALU.mult)
                stt(eng, y, t1, 1.5, y, ALU.add, ALU.mult)

        identb = A.alloc([128], BF16)
        identf = A.alloc([128])
        maskF = A.alloc([128]); maskB = A.alloc([128]); onesf = A.alloc([128])
        onesb = A.alloc([2], BF16)
        poolP = A.alloc([20, 128], BF16)
        par = A.alloc([PX])
        rowb = A.alloc([2048 + 512], BF16)
        sc_pp = A.alloc([8]); sc_pm = A.alloc([8]); hbm = A.alloc([16])
        xT = A.alloc([8, S], BF16)
        ybT = A.alloc([8, S], BF16)
        rstd = A.alloc([NT])
        g_ea = A.alloc([2, NT, 4]); g_ew = A.alloc([2, NT, 4]); g_cl = A.alloc([2, NT, 4]); g_dec = A.alloc([2, NT, 4])
        PH = A.top

        A.top = PH
        cst = A.alloc([24, 128])
        cvb32 = A.alloc([2048])
        dma("sp", cst.ap, cst_d.rearrange("p (a b) -> p a b", a=24), [], [cst])
        dma("sp", par.ap, par_d, [], [par])
        dma("sp", cvb32[0:1].ap, cvb_d, [], [cvb32])
        cp("dve", identb, cst[:, 0]); cp("dve", identf, cst[:, 0])
        cp("dve", maskF, cst[:, 1]); cp("dve", maskB, cst[:, 2]); cp("dve", onesf, cst[:, 3])
        memset("dve", onesb, 1.0)
        cp("act", poolP, cst[:, 4:24])
        cp("dve", rowb[0:1, 0:2048], cvb32[0:1])
        memset("dve", rowb[0:1, 2048:2560], 1.0)
        ts("dve", sc_pp, par[:, P_PSC:P_PSC + 8], 0.5, None, ALU.mult)
        ts("dve", sc_pm, par[:, P_MG:P_MG + 8], 0.25, None, ALU.mult)
        ts("dve", hbm, par[:, P_BM:P_BM + 16], 0.5, None, ALU.mult)

        NSLOT = 8
        stg = [A.alloc([1024]) for _ in range(NSLOT)]
        cbf = [A.alloc([1024], BF16) for _ in range(NSLOT)]
        it = [0]
        nstore = [0]

        class ConvPipe:
            def __init__(s, stg_, cbf_):
                s.stg, s.cbf, s.n, s.pending = stg_, cbf_, 0, []
            def load(s, item):
                src, c, col0, ncols, dsts, srcs_fn, scl, key, eng = item
                k = s.n % len(s.stg); s.n += 1
                st_, cb_ = s.stg[k][:, 0:ncols], s.cbf[k][:, 0:ncols]
                dma("sp", st_.ap, src[c * 128:(c + 1) * 128, col0:col0 + ncols], [], [st_])
                s.pending.append((st_, cb_, dsts, srcs_fn, scl, key, eng))
            def cast_store(s):
                if not s.pending:
                    return
                st_, cb_, dsts, srcs_fn, scl, key, eng = s.pending.pop(0)
                if eng == "act":
                    act(cb_, st_, AF.Identity, scale=scl)
                else:
                    ts("dve", cb_, st_, scl, None, ALU.mult)
                srcs = srcs_fn(cb_.ap)
                if not isinstance(dsts, list):
                    dsts, srcs = [dsts], [srcs]
                for dst, srcap in zip(dsts, srcs):
                    nstore[0] += 1
                    wk = [("wsc_up", nstore[0])] if key is None else [key]
                    sc.add("act" if key is None else "sp", lambda e, dst=dst, srcap=srcap: e.dma_start(out=dst, in_=srcap),
                           reads=[cb_], writes=wk, dma=True)

        gpre = lambda c: par[:, P_GPRE + c:P_GPRE + c + 1]
        def blocks_dest(first, stride, o0, n, c, jn=128):
            if stride == 1:
                v = wsc[first + o0:first + o0 + n]
            else:
                v = wsc[first:first + 8 * stride].rearrange("(o r) p e -> o r p e", r=stride)[o0:o0 + n, 0]
            return v[:, :, c * 128:c * 128 + jn].rearrange("o p j -> p o j")
        def srcn(n):
            return lambda ap: ap.rearrange("p (b j) -> p b j", b=n)
        def head_dsts(kind, c):
            return [wsc[B_HEAD + kind * 2 + h * 10:B_HEAD + kind * 2 + h * 10 + 2][:, :, c * 128:(c + 1) * 128]
                    .rearrange("o p j -> p o j") for h in range(4)]
        srch = lambda ap: [ap[:, h * 256:(h + 1) * 256].rearrange("p (b j) -> p b j", b=2) for h in range(4)]

        up = ConvPipe(stg, cbf)
        it = [0]
        def up_item(item):
            it[0] += 1
            up.load(item[:-1] + (("act", "dve")[it[0] % 2],))
            up.cast_store()
        for c in range(8):
            up_item((win_d, c, 7168, 16, blocks_dest(B_GATE, 1, 0, 1, c, 16), srcn(1), gpre(c), None, None))
        HK = ((K_Q, 2048), (K_K, 3072), (K_V, 4096), (K_O, 5120), (K_ZB, 6144))
        def head_item(h, kind, col, c, key, eng):
            dst = wsc[B_HEAD + kind * 2 + h * 10:B_HEAD + kind * 2 + h * 10 + 2][:, :, c * 128:(c + 1) * 128].rearrange("o p j -> p o j")
            return (win_d, c, col + h * 256, 256, dst, srcn(2), gpre(c), key, eng)
        for kind, col in HK:
            for c in range(8):
                up_item(head_item(0, kind, col, c, None, None))
        sc.add("sp", lambda e: None, reads=[("wsc_up", i_) for i_ in range(1, nstore[0] + 1)])

        deferred = []
        def defer(src, col0, first, stride, scale_fn, kind, nhalf=2, ncols=512, nblk=4):
            for c in range(8):
                for hh in range(nhalf):
                    deferred.append((src, c, col0 + hh * ncols, ncols, blocks_dest(first, stride, hh * nblk, nblk, c),
                                     srcn(nblk), scale_fn(c), ("wsc", kind), "dve"))
        for h_ in range(1, NH):
            for kind, col in HK:
                for c in range(8):
                    deferred.append(head_item(h_, kind, col, c, ("wsc", "head%d" % h_), "dve"))
        defer(win_d, 0, B_U, 1, gpre, "u")
        defer(wmix_d, 0, B_MIX, 1, lambda c: 1.0, "mix", nhalf=1, ncols=256, nblk=2)
        defer(wout_d, 0, B_OUT, 1, lambda c: 0.5, "out")
        defer(win_d, 1024, B_ZA, 1, gpre, "za")
        defer(wpp_d, 0, B_P3 + K_PP, 4, lambda c: sc_pp[:, c:c + 1], "pp")
        defer(win_d, 7184, B_P3 + K_GA, 4, gpre, "ga")
        defer(wpm_d, 0, B_P3 + K_PM, 4, lambda c: sc_pm[:, c:c + 1], "pm")
        defer(win_d, 8208, B_P3 + K_GBR, 4, gpre, "gb")
        dpipe = [None]
        def sprinkle(n=1):
            for _ in range(n):
                if dpipe[0] is None:
                    return
                if deferred:
                    dpipe[0].load(deferred.pop(0))
                if len(dpipe[0].pending) >= len(dpipe[0].stg) or not deferred:
                    dpipe[0].cast_store()
        def drain():
            while dpipe[0] is not None and (deferred or dpipe[0].pending):
                sprinkle()

        def wload(dst, b0, n, kinds=()):
            dma("sp", dst.ap.rearrange("p n c j -> p n (c j)"), wsc[b0:b0 + n].rearrange("n p e -> p n e"),
                [("wsc", k_) for k_ in kinds], [dst])

        import os
        STOP = int(os.environ.get('MK_STOP', '9'))
        for sq in range(NSEQ if STOP > 0 else 0):
            A.top = PH
            xn = [A.alloc([1024], BF16) for _ in range(3)]
            sqj = [A.alloc([1024], BF16) for _ in range(2)]
            ss4 = [A.alloc([4]) for _ in range(4)]; rt14 = [A.alloc([4]) for _ in range(4)]; ssx4 = [A.alloc([4]) for _ in range(4)]
            wg = A.alloc([1, 8, 128], BF16)
            gsb = A.alloc([NT, 16])
            lfn = A.alloc([2, NT, 4]); gt1 = A.alloc([2, NT, 4]); gt2 = A.alloc([2, NT, 4])
            bnS = A.alloc([2, NT, 4]); tnS = A.alloc([2, NT, 4])
            A.top = max(A.top, PH + NT * 512 + 4096)
            xall = [A.alloc([1024]) for _ in range(NT)]
            dma("sp", wg[:, 0, :, 0:16].ap, wsc[B_GATE].rearrange("p (c j) -> p c j", c=8)[:, :, 0:16], [], [wg])
            G4 = 4 if NT % 4 == 0 else NT
            for t in range(NT):
                dma("sp", xall[t].ap, x_d[sq, t * 128:(t + 1) * 128, :], [], [xall[t]])
                act(sqj[t % 2], xall[t], AF.Square, accum=ss4[t // G4][:, t % G4:t % G4 + 1])
            for g4 in range(NT // G4):
                ts("dve", ssx4[g4], ss4[g4], 1.0 / D, EPS, ALU.mult, ALU.add)
                rsqrt("dve", rstd[:, g4 * G4:(g4 + 1) * G4], ssx4[g4], rt14[g4])
                for t in range(g4 * G4, (g4 + 1) * G4):
                    k = t % 3
                    ts("dve", xn[k], xall[t], rstd[:, t:t + 1], None, ALU.mult)
                    pb = psv(t % 2, BF16)
                    for c in range(8):
                        tr(pb[:, c * 128:(c + 1) * 128], xn[k][:, c * 128:(c + 1) * 128], identb)
                    o_ = xT[:, :, t * 128:(t + 1) * 128]
                    i_ = pb
                    eng = "act" if t % 2 == 0 else "dve"
                    if eng == "act":
                        sc.add("act", lambda e, o_=o_, i_=i_: e.copy(o_.ap, i_.ap.rearrange("p (c j) -> p c j", c=8)),
                               reads=[i_], writes=[o_])
                    else:
                        sc.add("dve", lambda e, o_=o_, i_=i_: e.tensor_copy(o_.ap, i_.ap.rearrange("p (c j) -> p c j", c=8)),
                               reads=[i_], writes=[o_])
            if STOP == 1 and int(os.environ.get('MK_SUB', '9')) == 0:
                continue
            pg = psv(2)
            for t in range(NT):
                mm(pg[:, t * 16:(t + 1) * 16],
                   [(xT[:, c, t * 128:(t + 1) * 128], wg[:, 0, c, 0:16]) for c in range(8)])
            gbias = par[:, P_GB:P_GB + 16]
            gsf = gsb[:, :, 0:8]
            sc.add("dve", lambda e: e.tensor_tensor(gsf.ap, pg[:, 0:NT * 16].ap.rearrange("p (t k) -> p t k", k=16)[:, :, 0:8],
                                                    gbias[:, 0:8].ap.unsqueeze(1).broadcast_to([128, NT, 8]), ALU.add),
                   reads=[pg[:, 0:NT * 16], gbias], writes=[gsf])
            for t in range(NT):
                tt("dve", gsb[:, NT - 1 - t, 8:16], pg[:, t * 16 + 8:t * 16 + 16], gbias[:, 8:16], ALU.add)
            for dr in range(2):
                act(lfn[:, dr], gsb[:, :, dr * 8 + 4:dr * 8 + 8], AF.Exp, scale=-1.0)
            ts("dve", lfn, lfn, 1.0, None, ALU.add)
            act(lfn, lfn, AF.Ln)
            pc_ = psv(3)
            def flat(v_):
                return V(v_.ap.rearrange("p t h -> p (t h)"), v_.space, v_.off, [128, NT * 4], [0, 4], 4)
            for dr in range(2):
                rflat = flat(lfn[:, dr])
                mm(pc_[:, dr * 64:dr * 64 + NT * 4], [((maskF, maskB)[dr], rflat)])
                mm(pc_[:, 128 + dr * 64:128 + dr * 64 + NT * 4], [(onesf, rflat)])
            for dr in range(2):
                cp("dve", flat(bnS[:, dr]), pc_[:, dr * 64:dr * 64 + NT * 4])
                cp("dve", flat(tnS[:, dr]), pc_[:, 128 + dr * 64:128 + dr * 64 + NT * 4])
            for dr in range(2):
                ipre = gsb[:, :, dr * 8:dr * 8 + 4]
                o1 = gt1[:, dr]; o2 = gt2[:, dr]
                tt("dve", o1, ipre, bnS[:, dr], ALU.add)
                tt("dve", o2, o1, tnS[:, dr], ALU.subtract)
                act(g_ea[:, dr], o1, AF.Exp)
                act(g_ew[:, dr], o2, AF.Exp)
                act(g_cl[:, dr], bnS[:, dr], AF.Exp)
                ts("dve", g_cl[:, dr], g_cl[:, dr], 64.0, None, ALU.mult)
                act(g_dec[:, dr], tnS[:, dr], AF.Exp, scale=-1.0)

            if STOP == 1:
                continue
            A.top = PH
            wh = A.alloc([10, 8, 128], BF16)
            qkpre = A.alloc([4, S + 4], BF16)
            gate_all = V(arena[:, qkpre.off // 4:qkpre.off // 4 + NT * 256].rearrange("p (a b) -> p a b", a=NT),
                         "sb", qkpre.off, [128, NT, 256], [0, 1024, 4], 4)
            diag = A.alloc([20, 128], BF16)
            qT2 = A.alloc([2, S], BF16); kT2 = A.alloc([2, S], BF16)
            ktok = A.alloc([NT, 256], BF16); vtok = A.alloc([NT, 257], BF16)
            hfirst = A.alloc([NT, 256])
            Cst = [A.alloc([2, 2, 256]) for _ in range(2)]
            nst = [A.alloc([2, 2]) for _ in range(2)]
            Cb = [A.alloc([2, 2, 257], BF16) for _ in range(2)]
            Sp = [[A.alloc([128], BF16) for _ in range(4)] for _ in range(2)]
            ksc = [[A.alloc([256], BF16) for _ in range(2)] for _ in range(2)]
            tnh = [A.alloc([512]) for _ in range(2)]
            ga_ = [A.alloc([256])] * 2
            junk = [A.alloc([256], BF16) for _ in range(2)] * 2
            yb = [A.alloc([256], BF16) for _ in range(2)]
            ssh = A.alloc([NT]); sxh = A.alloc([NT]); rsh = A.alloc([NT]); t1h = A.alloc([NT])
            dn2 = [A.alloc([2, 1]) for _ in range(2)]
            ad2 = [A.alloc([2, 1]) for _ in range(2)]
            rd2 = [A.alloc([2, 1]) for _ in range(2)]
            memset("pool", vtok[:, :, 256:257], 1.0)
            if sq == 0:
                dpipe[0] = ConvPipe([A.alloc([512]) for _ in range(2)], [A.alloc([512], BF16) for _ in range(2)])
            build.dbg = dict(ybT=ybT, hfirst=hfirst, gate_all=gate_all, rsh=rsh, ssh=ssh, sxh=sxh, vtok=vtok, g_cl=g_cl, g_ea=g_ea, g_ew=g_ew, g_dec=g_dec, rd2=rd2[0], ad2=ad2[0], dn2=dn2[0], qT2=qT2, kT2=kT2, ktok=ktok)
            pcount = [0]
            HALF = NT // 2

            def flat2(v_, n, esz):
                return V(v_.ap.rearrange("p a b -> p (a b)"), v_.space, v_.off, [128, n], [0, esz], esz)

            for h in range(NH):
                if sq == 0 and h >= 1 and dpipe[0] is not None:
                    hk_ = ("wsc", "head%d" % h)
                    while any(it_[7] == hk_ for it_ in deferred) or any(p_[5] == hk_ for p_ in dpipe[0].pending):
                        sprinkle()
                wload(wh, B_HEAD + h * 10, 10, ["head%d" % h] if h >= 1 else [])
                memset("pool", qkpre[:, :, 0:2], 0.0)
                memset("pool", qkpre[:, :, S + 2:S + 4], 0.0)
                for ci in range(4):
                    cg = (0 if ci < 2 else 8) + h * 2 + (ci % 2)
                    for j in range(5):
                        ts("dve", diag[:, ci * 5 + j], identf, par[:, P_CW + cg * 5 + j:P_CW + cg * 5 + j + 1], None, ALU.mult)
                for ci in range(4):
                    for g in range(NG):
                        pb = psv(pcount[0] % 2); pcount[0] += 1
                        mm(pb, [(wh[:, ci, c, :], xT[:, c, g * 512:(g + 1) * 512]) for c in range(8)])
                        cp("act", qkpre[:, ci, 2 + g * 512:2 + (g + 1) * 512], pb)
                        if sq == 0:
                            sprinkle()
                for ci in range(4):
                    cg = (0 if ci < 2 else 8) + h * 2 + (ci % 2)
                    for g in range(NG):
                        pb = psv(2 + pcount[0] % 2); pcount[0] += 1
                        pairs = [(diag[:, ci * 5 + j], qkpre[:, ci, g * 512 + j:g * 512 + j + 512]) for j in range(5)]
                        pairs.append((rowb[0:1, cg * 128:(cg + 1) * 128], rowb[0:1, 2048:2560]))
                        mm(pb, pairs)
                        tb = tnh[pcount[0] % 2]
                        act(tb, pb, AF.Tanh, scale=0.5)
                        dst = (qT2 if ci < 2 else kT2)[:, ci % 2, g * 512:(g + 1) * 512]
                        stt("dve", dst, tb, 1.0, pb, ALU.add, ALU.mult)
                        if sq == 0:
                            sprinkle()
                for t in range(NT):
                    tc_ = slice(t * 128, (t + 1) * 128)
                    pb = psv(4 + t % 2)
                    mm(pb[:, 0:256], [(xT[:, c, tc_], wh[:, 4:6, c, :]) for c in range(8)])
                    cp("act", vtok[:, t, 0:256], pb[:, 0:256])
                    pk = psv(6, BF16)[:, (t % 4) * 256:(t % 4) * 256 + 256]
                    for j in range(2):
                        tr(pk[:, j * 128:(j + 1) * 128], kT2[:, j, tc_], identb)
                    cp("dve", ktok[:, t], pk)
                    pgate = psv(7 if t % 2 == 0 else 3)
                    mm(pgate, [(xT[:, c, tc_], wh[:, 6:10, c, :]) for c in range(8)])
                    tb = tnh[t % 2]
                    act(tb, pgate, AF.Tanh, scale=0.5)
                    a_ = ga_[t % 2]
                    stt("dve", a_, tb[:, 256:512], 1.0, pgate[:, 256:512], ALU.add, ALU.mult)
                    stt("dve", gate_all[:, t], tb[:, 0:256], 1.0, a_, ALU.add, ALU.mult)
                    if sq == 0:
                        sprinkle()
                memset("pool", Cst[1], 0.0); memset("pool", nst[1], 0.0); memset("pool", Cb[1], 0.0)
                psS = [psv(0)[:, 0:128], psv(0)[:, 128:256]]
                psP = [psv(2), psv(3)]
                psN = [psv(5)[:, 0:2], psv(5)[:, 64:66]]
                numbanks = [psv(6), psv(7), psv(1), psv(4)]
                NB_ = len(numbanks)
                def cflat(v_):
                    return V(v_.ap.rearrange("p a b -> p (a b)"), "sb", v_.off, [128, 512], [0, 4], 4)

                def tile_of(i, dr):
                    return i if dr == 0 else NT - 1 - i

                def stage_s(i):
                    for dr in range(2):
                        t = tile_of(i, dr)
                        tc_ = slice(t * 128, (t + 1) * 128)
                        mm(psS[dr], [(kT2[:, c, tc_], qT2[:, c, tc_]) for c in range(2)])
                    for dr in range(2):
                        sraw = yb[dr][:, (i % 2) * 128:(i % 2 + 1) * 128]
                        act(sraw, psS[dr], AF.Identity, scale=g_ea[:, dr, i, h:h + 1])
                        tt("pool", Sp[dr][i % 4], sraw, (maskF, maskB)[dr], ALU.mult)

                def stage_k(i):
                    for dr in range(2):
                        t = tile_of(i, dr)
                        act(ksc[dr][i % 2], ktok[:, t], AF.Identity, scale=g_ew[:, dr, i, h:h + 1])

                def stage_p(i):
                    for dr in range(2):
                        t = tile_of(i, dr)
                        ks = ksc[dr][i % 2]
                        for dc in range(2):
                            mm(psP[dr][:, dc * 256:(dc + 1) * 256], [(ks[:, dc * 128:(dc + 1) * 128], vtok[:, t, 0:256])])
                        for dc in range(2):
                            mm(psN[dr][:, dc:dc + 1], [(ks[:, dc * 128:(dc + 1) * 128], vtok[:, t, 256:257])])

                stage_k(0)
                for i0 in range(min(3, NT)):
                    stage_s(i0)
                stage_p(0)
                def den_evac(i):
                    k2 = i % 2
                    psNum = [numbanks[(2 * i) % NB_], numbanks[(2 * i + 1) % NB_]]
                    tt("dve", ad2[k2], dn2[k2], g_cl[:, :, i, h:h + 1], ALU.max)
                    sc.add("dve", lambda e, o=rd2[k2], a=ad2[k2]: e.reciprocal(o.ap, a.ap), reads=[ad2[k2]], writes=[rd2[k2]])
                    for dr in range(2):
                        t = tile_of(i, dr)
                        if i < HALF:
                            act(hfirst[:, t], psNum[dr][:, 0:256], AF.Identity, scale=rd2[k2][:, dr])
                        else:
                            htmp = tnh[k2][:, dr * 256:(dr + 1) * 256]
                            act(htmp, psNum[dr][:, 0:256], AF.Identity, scale=rd2[k2][:, dr])
                            tt("pool", hfirst[:, t], hfirst[:, t], htmp, ALU.add)
                            if i >= HALF + 1:
                                tp_ = tile_of(i - 1, dr)
                                jk_ = junk[(2 * i + dr) % 4]; hh_ = hfirst[:, tp_]; acc_ = ssh[:, tp_:tp_ + 1]
                                sc.add("dve", lambda e, jk_=jk_, hh_=hh_, acc_=acc_: e.scalar_tensor_tensor(
                                    jk_.ap, hh_.ap, 1.0, hh_.ap, ALU.mult, ALU.mult, accum_out=acc_.ap),
                                    reads=[hh_], writes=[jk_, acc_])

                for i in range(NT):
                    cbr = Cb[(i + 1) % 2]
                    cbw = Cb[i % 2]
                    psNum = [numbanks[(2 * i) % NB_], numbanks[(2 * i + 1) % NB_]]
                    if i + 1 < NT:
                        stage_k(i + 1)
                    if i >= 1:
                        den_evac(i - 1)
                    for dr in range(2):
                        t = tile_of(i, dr)
                        tc_ = slice(t * 128, (t + 1) * 128)
                        mm(psNum[dr][:, 0:257], [(Sp[dr][i % 4], vtok[:, t])] +
                           [(qT2[:, c, tc_], cbr[:, dr, c]) for c in range(2)])
                    for dr in range(2):
                        act(dn2[i % 2][:, dr], psNum[dr][:, 256:257], AF.Abs)
                    for dr in range(2):
                        stt("dve", cflat(Cst[i % 2][:, dr]), cflat(Cst[(i + 1) % 2][:, dr]), g_dec[:, dr, i, h:h + 1],
                            psP[dr], ALU.mult, ALU.add)
                        stt("dve", nst[i % 2][:, dr], nst[(i + 1) % 2][:, dr], g_dec[:, dr, i, h:h + 1],
                            psN[dr], ALU.mult, ALU.add)
                    cp("act", cbw[:, 0, :, 0:256], Cst[i % 2][:, 0])
                    cp("dve", cbw[:, 1, :, 0:256], Cst[i % 2][:, 1])
                    cp("dve", cbw[:, :, :, 256], nst[i % 2])
                    if sq == 0:
                        sprinkle()
                    if i + 3 < NT:
                        stage_s(i + 3)
                    if i + 1 < NT:
                        stage_p(i + 1)
                den_evac(NT - 1)
                for dr in range(2):
                    tp_ = tile_of(NT - 1, dr)
                    act(junk[dr], hfirst[:, tp_], AF.Square, accum=ssh[:, tp_:tp_ + 1])
                ts("dve", sxh, ssh, 1.0 / DH, EPS, ALU.mult, ALU.add)
                rsqrt("dve", rsh, sxh, t1h)
                for t in range(NT):
                    tc_ = slice(t * 128, (t + 1) * 128)
                    ybb = yb[t % 2]
                    stt("dve", ybb, hfirst[:, t], rsh[:, t:t + 1], gate_all[:, t], ALU.mult, ALU.mult)
                    pT = psv(t % 2, BF16)[:, 0:256]
                    for j in range(2):
                        tr(pT[:, j * 128:(j + 1) * 128], ybb[:, j * 128:(j + 1) * 128], identb)
                    o_ = ybT[:, h * 2:h * 2 + 2, tc_]
                    sc.add("act", lambda e, o_=o_, pT=pT: e.copy(o_.ap, pT.ap.rearrange("p (c j) -> p c j", c=2)),
                           reads=[pT], writes=[o_])

            if sq == 0:
                drain()
                dpipe[0] = None
            if STOP == 2:
                continue
            A.top = PH
            utok = A.alloc([NT, 1024], BF16)
            wu = A.alloc([8, 8, 128], BF16)
            pooledT = A.alloc([8, S], BF16)
            wmx = A.alloc([2, 8, 128], BF16)
            wout = A.alloc([8, 8, 128], BF16)
            TOP2 = A.top
            OVL = (NT * 512 + 4096) >= 11264
            if OVL:
                A.top = PH
            wz = [A.alloc([1, 8, 128], BF16) for _ in range(2)]
            w3 = [A.alloc([4, 8, 128], BF16) for _ in range(2)]
            yaT = A.alloc([8, 512], BF16)
            mgT = A.alloc([8, 512], BF16)
            tz = [A.alloc([512]) for _ in range(2)]
            sz = [A.alloc([512]) for _ in range(2)]
            xre1 = A.alloc([1024])
            if OVL:
                assert A.top <= PH + NT * 512 + 4096
                A.top = TOP2
            tA = [A.alloc([512]) for _ in range(2)]
            tB = [A.alloc([512]) for _ in range(2)]
            xre = [A.alloc([1024]), xre1]
            yo = [A.alloc([1024]) for _ in range(2)]
            ssA = [A.alloc([1]) for _ in range(2)]
            ssB = [A.alloc([1]) for _ in range(2)]
            sm3 = A.alloc([64])
            sm3i = [0]
            def smc3():
                k = sm3i[0]; sm3i[0] = (k + 1) % 64
                return sm3[:, k:k + 1]
            wload(wu, B_U, 8, ["u"]); wload(wmx, B_MIX, 2, ["mix"]); wload(wout, B_OUT, 8, ["out"])
            for t in range(NT):
                for hf in range(2):
                    pb = psv((t % 2) * 2 + hf)
                    mm(pb, [(xT[:, c, t * 128:(t + 1) * 128], wu[:, hf * 4:(hf + 1) * 4, c, :]) for c in range(8)])
                    cp("act" if hf == 0 else "dve", utok[:, t, hf * 512:(hf + 1) * 512], pb)
            pcn = [0]
            for g in range(NG):
                for cc in range(8):
                    pgp = cc // 2
                    pb = psv(4 + pcn[0] % 2); pcn[0] += 1
                    for tl in range(4):
                        t = g * 4 + tl
                        pairs = []
                        if t > 0:
                            pairs.append((utok[:, t - 1, cc * 128:(cc + 1) * 128], poolP[:, pgp * 5 + 0]))
                        var = 2
                        if t == 0:
                            var = 3
                        elif t == NT - 1:
                            var = 4
                        pairs.append((utok[:, t, cc * 128:(cc + 1) * 128], poolP[:, pgp * 5 + var]))
                        if t < NT - 1:
                            pairs.append((utok[:, t + 1, cc * 128:(cc + 1) * 128], poolP[:, pgp * 5 + 1]))
                        mm(pb[:, tl * 128:(tl + 1) * 128], pairs)
                    cp("act", pooledT[:, cc, g * 512:(g + 1) * 512], pb)
            def sweep1(g, ocs):
                gc = slice(g * 512, (g + 1) * 512)
                for oc in ocs:
                    k = oc % 2
                    w_ = wz[k]
                    wload(w_, B_ZA + oc, 1, ["za"])
                    pgp, oh = oc // 2, oc % 2
                    psM = psv(0 + k); psZ = psv(2 + k)
                    mm(psM, [(wmx[:, oh, pgp * 2 + kc, :], pooledT[:, pgp * 2 + kc, gc]) for kc in range(2)])
                    mm(psZ, [(w_[:, 0, c, :], xT[:, c, gc]) for c in range(8)])
                    act(tz[k], psZ, AF.Tanh, scale=0.5)
                    stt("dve", sz[k], tz[k], 1.0, psZ, ALU.add, ALU.mult)
                    tt("dve", yaT[:, oc], psM, sz[k], ALU.mult)

            def sweep2(g):
                gc = slice(g * 512, (g + 1) * 512)
                for oc in range(8):
                    k = oc % 2
                    w_ = w3[k]
                    wload(w_, B_P3 + oc * 4, 4, ["pp", "ga", "pm", "gb"])
                    psPA = psv(4 * k + 0); psGA = psv(4 * k + 1); psPB = psv(4 * k + 2); psGB = psv(4 * k + 3)
                    mm(psGA, [(w_[:, K_GA, c, :], xT[:, c, gc]) for c in range(8)])
                    mm(psGB, [(w_[:, K_GBR, c, :], xT[:, c, gc]) for c in range(8)])
                    mm(psPA, [(w_[:, K_PP, c, :], yaT[:, c, :]) for c in range(8)])
                    mm(psPB, [(w_[:, K_PM, c, :], ybT[:, c, gc]) for c in range(8)])
                    act(tA[k], psGA, AF.Tanh, scale=0.5, bias=hbm[:, oc:oc + 1])
                    act(tB[k], psGB, AF.Tanh, scale=0.5, bias=hbm[:, 8 + oc:8 + oc + 1])
                    stt("dve", tA[k], tA[k], 1.0, psPA, ALU.add, ALU.mult)
                    stt("dve", tB[k], tB[k], 1.0, psPB, ALU.add, ALU.mult)
                    tt("pool", mgT[:, oc], tA[k], tB[k], ALU.add)

            def out_tile(g, tl):
                t = g * 4 + tl
                k = tl % 2
                po = [psv(4 + 2 * k), psv(5 + 2 * k)]
                for hf in range(2):
                    mm(po[hf], [(mgT[:, c, tl * 128:(tl + 1) * 128], wout[:, hf * 4:(hf + 1) * 4, c, :]) for c in range(8)])
                dma("sp", xre[k].ap, x_d[sq, t * 128:(t + 1) * 128, :], [], [xre[k]])
                yob = V(yo[k].ap.bitcast(BF16), "sb", yo[k].off, [128, 2048], [0, 2], 2)
                act(yob[:, 0:512], po[0], AF.Square, scale=float(D ** -0.5), accum=ssA[k])
                act(yob[:, 512:1024], po[1], AF.Square, scale=float(D ** -0.5), accum=ssB[k])
                sx = smc3(); rs = smc3(); t1 = smc3()
                stt("dve", sx, ssA[k], EPS, ssB[k], ALU.add, ALU.add)
                rsqrt("dve", rs, sx, t1)
                for hf in range(2):
                    stt("dve", yo[k][:, hf * 512:(hf + 1) * 512], po[hf], rs,
                        par[:, P_GPOST + hf * 512:P_GPOST + (hf + 1) * 512], ALU.mult, ALU.mult)
                tt("pool", yo[k], yo[k], xre[k], ALU.add)
                dma("pool", y_d[sq, t * 128:(t + 1) * 128, :], yo[k].ap, [yo[k]], [("y", sq, t)])

            sweep1(0, range(8))
            for g in range(NG):
                sweep2(g)
                if g + 1 < NG:
                    out_tile(g, 0); out_tile(g, 1)
                    sweep1(g + 1, range(0, 4))
                    out_tile(g, 2); out_tile(g, 3)
                    sweep1(g + 1, range(4, 8))
                else:
                    for tl in range(4):
                        out_tile(g, tl)

        sc.add("pool", lambda e: None, reads=[("y", sq, t) for sq in range(NSEQ) for t in range(NT)])

        sc.finalize(csems, dsems)

        @block.sync
        def _(e):
            sc.emit("sp", e)

        @block.tensor
        def _(e):
            sc.emit("pe", e)

        @block.scalar
        def _(e):
            sc.emit("act", e)

        @block.vector
        def _(e):
            sc.emit("dve", e)

        @block.gpsimd
        def _(e):
            sc.emit("pool", e)

    return nc


_CACHE = {}


def _prep_shared(inp, S):
    return {
        "w_in": np.ascontiguousarray(np.asarray(inp["w_in"][0], np.float32)),
        "w_pp": np.ascontiguousarray(np.asarray(inp["w_proj_pool"][0], np.float32)),
        "w_pm": np.ascontiguousarray(np.asarray(inp["w_proj_mlstm"][0], np.float32)),
        "w_out": np.ascontiguousarray(np.asarray(inp["w_out"][0], np.float32)),
        "w_mix": np.ascontiguousarray(np.asarray(inp["pool_mix_w"][0], np.float32).reshape(1024, 256)),
        "consts": _consts(S),
        "params": _params(inp),
        "convb": np.ascontiguousarray(np.asarray(inp["conv_qk_b"][0], np.float32).reshape(1, 2048)),
    }


def run(inp, n_cores, nseq):
    x = np.asarray(inp["x"], np.float32)
    B, S, _ = x.shape
    assert B == n_cores * nseq
    key = (S, nseq)
    if key not in _CACHE:
        _CACHE[key] = build(S, nseq)
    nc = _CACHE[key]
    shared = _prep_shared(inp, S)
    in_maps = []
    for c in range(n_cores):
        m = dict(shared)
        m["x"] = np.ascontiguousarray(x[c * nseq:(c + 1) * nseq])
        in_maps.append(m)
    res = run_bass_kernel_spmd(nc, in_maps, core_ids=list(range(n_cores)))
    return np.concatenate([np.asarray(r["y"], np.float32) for r in res.results], axis=0)


def kernel(**inputs):
    return run(inputs, 8, 2)
```

```python
import numpy as np
import ml_dtypes
import concourse.bass as bass
import concourse.mybir as mybir
from concourse.bass_utils import run_bass_kernel_spmd

F32 = mybir.dt.float32
BF16 = mybir.dt.bfloat16
I32 = mybir.dt.int32
AF = mybir.ActivationFunctionType
ALU = mybir.AluOpType

D = 1024
NH = 4
DH = 256
EPS = 1e-6
BLK = 256


class V:
    def __init__(s, ap, space, off, shape, strides, esz):
        s.ap, s.space, s.off, s.shape, s.strides, s.esz = ap, space, off, list(shape), list(strides), esz
        s._b = None

    def __getitem__(s, idx):
        if not isinstance(idx, tuple):
            idx = (idx,)
        idx = list(idx) + [slice(None)] * (len(s.shape) - len(idx))
        off = s.off
        shape, strides = [], []
        for d, (i, n) in enumerate(zip(idx, s.shape)):
            if isinstance(i, int):
                if d == 0:
                    shape.append(1); strides.append(0)
                    idx[0] = slice(i, i + 1)
                else:
                    off += i * s.strides[d]
            else:
                a = 0 if i.start is None else i.start
                b = n if i.stop is None else i.stop
                assert 0 <= a < b <= n, (a, b, n)
                if d > 0:
                    off += a * s.strides[d]
                shape.append(b - a); strides.append(s.strides[d])
        return V(s.ap[tuple(idx)], s.space, off, shape, strides, s.esz)

    def blocks(s):
        if s._b is None and s.space == "ps":
            s._b = {("ps", s.off // 2048)}
        if s._b is None:
            res = set()
            fs, st = s.shape[1:], s.strides[1:]
            if not fs:
                fs, st = [1], [s.esz]
            run = fs[-1] * st[-1]
            outer = fs[:-1]
            idxs = [0] * len(outer)
            while True:
                o = s.off + sum(i * t for i, t in zip(idxs, st[:-1]))
                for b in range(o // BLK, (o + run - 1) // BLK + 1):
                    res.add((s.space, b))
                k = len(outer) - 1
                while k >= 0:
                    idxs[k] += 1
                    if idxs[k] < outer[k]:
                        break
                    idxs[k] = 0
                    k -= 1
                if k < 0:
                    break
            s._b = res
        return s._b


def _keys(lst):
    out = set()
    for x in lst:
        if isinstance(x, V):
            out |= x.blocks()
        else:
            out.add(x)
    return out


class Op:
    __slots__ = ("eng", "emit", "deps", "sig", "sem", "val", "dma")

    def __init__(s, eng, emit, dma):
        s.eng, s.emit, s.dma = eng, emit, dma
        s.deps = set(); s.sig = dma; s.sem = None; s.val = 0


class Sched:
    ENGS = ("pe", "act", "dve", "pool", "sp")

    def __init__(s):
        s.ops = {e: [] for e in s.ENGS}
        s.lw, s.rd = {}, {}

    def _dep(s, op, d, kind):
        if d is op:
            return
        if d.eng == op.eng and not d.dma and not op.dma:
            if op.eng == "pe":
                return
        op.deps.add(d); d.sig = True

    def add(s, eng, emit, reads=(), writes=(), dma=False):
        op = Op(eng, emit, dma)
        rk, wk = _keys(reads), _keys(writes)
        for k in rk:
            w = s.lw.get(k)
            if w is not None:
                s._dep(op, w, "raw")
        for k in wk:
            w = s.lw.get(k)
            if w is not None:
                s._dep(op, w, "waw")
            for r in s.rd.get(k, ()):
                s._dep(op, r, "war")
        for k in rk:
            s.rd.setdefault(k, []).append(op)
        for k in wk:
            s.lw[k] = op; s.rd[k] = []
        s.ops[eng].append(op)
        return op

    def finalize(s, sems, dsems):
        for e in s.ENGS:
            cnt = 0
            pool = dsems.get(e, [])
            use = [0] * len(pool)
            last = [None] * len(pool)
            nd = 0
            for op in s.ops[e]:
                if op.dma:
                    j = nd % len(pool); nd += 1
                    use[j] += 1
                    op.sem, op.val = pool[j], 16 * use[j]
                    if last[j] is not None:
                        op.deps.add(last[j])
                    last[j] = op
                elif op.sig:
                    cnt += 1
                    op.sem, op.val = sems[e], cnt

    def emit(s, e, eng):
        waited = {}
        for op in s.ops[e]:
            need = {}
            for d in op.deps:
                k = d.sem.num
                if need.get(k, (None, 0))[1] < d.val:
                    need[k] = (d.sem, d.val)
            for k, (sem, val) in need.items():
                if waited.get(k, 0) < val:
                    eng.wait_ge(sem, val); waited[k] = val
            ins = op.emit(eng)
            if op.sig and ins is not None:
                ins.then_inc(op.sem, 16 if op.dma else 1)


POOL_WINDOWS = (2, 4, 8, 16)


def _pool_mats(S):
    out = np.zeros((4, 5, 128, 128), np.float32)
    for g, w in enumerate(POOL_WINDOWS):
        def full(t_glob0, tp_glob0, S=S):
            M = np.zeros((128, 128), np.float32)
            for t in range(128):
                tg = t_glob0 + t
                lo = min(max(tg - w // 2, 0), S - 1)
                hi = min(max(tg + (w - 1 - w // 2), 0), S - 1)
                cnt = hi - lo + 1
                for tp in range(128):
                    tpg = tp_glob0 + tp
                    v = 0.0
                    if lo <= tpg <= hi:
                        v += 1.0 / cnt
                    if tpg == tg:
                        v -= 1.0
                    M[tp, t] = v
            return M
        mid = 128 * 8
        out[g, 0] = full(mid, mid - 128, 1 << 20)
        out[g, 1] = full(mid, mid + 128, 1 << 20)
        out[g, 2] = full(mid, mid, 1 << 20)
        out[g, 3] = full(0, 0)
        out[g, 4] = full(S - 128, S - 128)
    return out


def _consts(S):
    ident = np.eye(128, dtype=np.float32)
    s = np.arange(128)
    maskF = (s[:, None] <= s[None, :]).astype(np.float32)
    maskB = (s[:, None] >= s[None, :]).astype(np.float32)
    ones = np.ones((128, 128), np.float32)
    pm = _pool_mats(S).reshape(20, 128, 128)
    allm = np.concatenate([ident[None], maskF[None], maskB[None], ones[None], pm], 0)
    return np.ascontiguousarray(allm.transpose(1, 0, 2).reshape(128, 24 * 128))


P_GPRE, P_PSC, P_MG, P_CW, P_BM, P_GB, P_GPOST = 0, 8, 16, 24, 104, 120, 136
PX = 136 + 1024


def _params(inp):
    p = np.zeros((128, PX), np.float32)
    def pc(v):
        return np.asarray(v, np.float32).reshape(8, 128).T
    p[:, P_GPRE:P_GPRE + 8] = pc(inp["norm_pre_g"][0])
    p[:, P_PSC:P_PSC + 8] = pc(inp["pool_scale"][0])
    p[:, P_MG:P_MG + 8] = pc(inp["mlstm_norm_g"][0])
    cw = np.asarray(inp["conv_qk_w"][0], np.float32)
    p[:, P_CW:P_CW + 80] = cw.reshape(5, 16, 128).transpose(2, 1, 0).reshape(128, 80)
    bm = np.asarray(inp["b_merge"][0], np.float32)
    p[:, P_BM:P_BM + 16] = bm.reshape(2, 8, 128).transpose(2, 0, 1).reshape(128, 16)
    gb = np.zeros((2, 2, 4), np.float32)
    gb[:, 0, :] = np.asarray(inp["b_igate"][0], np.float32)
    gb[:, 1, :] = np.asarray(inp["b_fgate"][0], np.float32)
    p[:, P_GB:P_GB + 16] = np.broadcast_to(gb.reshape(1, 16), (128, 16))
    p[:, P_GPOST:P_GPOST + 1024] = np.broadcast_to(np.asarray(inp["norm_post_g"][0], np.float32)[None, :], (128, 1024))
    return p


B_U, B_HEAD, B_GATE, B_ZA, B_P3, B_OUT, B_MIX, NB = 0, 8, 48, 49, 57, 89, 97, 99
K_Q, K_K, K_V, K_O, K_ZB = 0, 1, 2, 3, 4
K_PP, K_GA, K_PM, K_GBR = 0, 1, 2, 3


def build(S, NSEQ):
    NT = S // 128
    NG = S // 512
    nc = bass.Bass("TRN2", target_bir_lowering=False)
    dt = lambda n, sh, ty, k: nc.dram_tensor(n, sh, ty, kind=k).ap()
    x_d = dt("x", [NSEQ, S, D], F32, "ExternalInput")
    win_d = dt("w_in", [D, 9232], F32, "ExternalInput")
    wpp_d = dt("w_pp", [D, D], F32, "ExternalInput")
    wpm_d = dt("w_pm", [D, D], F32, "ExternalInput")
    wout_d = dt("w_out", [D, D], F32, "ExternalInput")
    wmix_d = dt("w_mix", [D, 256], F32, "ExternalInput")
    cst_d = dt("consts", [128, 24 * 128], F32, "ExternalInput")
    par_d = dt("params", [128, PX], F32, "ExternalInput")
    cvb_d = dt("convb", [1, 2048], F32, "ExternalInput")
    y_d = dt("y", [NSEQ, S, D], F32, "ExternalOutput")
    wsc = dt("wsc", [NB, 128, 1024], BF16, "Internal")

    sc = Sched()
    AW = 212000 // 4 // 64 * 64

    import contextlib
    with contextlib.ExitStack() as es:
        arena = es.enter_context(nc.sbuf_tensor("arena", [128, AW], F32))
        banks = [es.enter_context(nc.psum_tensor("ps%d" % i, [128, 512], F32)) for i in range(8)]
        csems = {e: es.enter_context(nc.semaphore("s_" + e)) for e in ("pe", "act", "dve", "pool")}
        dsems = {"sp": [es.enter_context(nc.semaphore("d_sp%d" % i)) for i in range(12)],
                 "act": [es.enter_context(nc.semaphore("d_act%d" % i)) for i in range(6)],
                 "pool": [es.enter_context(nc.semaphore("d_pool%d" % i)) for i in range(6)]}
        block = es.enter_context(nc.Block())

        class Arena:
            def __init__(s): s.top = 0
            def alloc(s, shape, ty=F32, at=None):
                esz = 4 if ty in (F32, I32) else 2
                n = int(np.prod(shape))
                words = (n * esz + 3) // 4
                words = (words + 63) // 64 * 64
                if at is None:
                    at = s.top; s.top += words
                assert at + words <= AW, ("arena overflow", at, words, AW)
                ap = arena[:, at:at + words]
                if ty != F32:
                    ap = ap.bitcast(ty)
                ap = ap[:, 0:n]
                if len(shape) == 2:
                    names = "p (a b) -> p a b"; ap = ap.rearrange(names, a=shape[0])
                elif len(shape) == 3:
                    ap = ap.rearrange("p (a b c) -> p a b c", a=shape[0], b=shape[1])
                elif len(shape) == 4:
                    ap = ap.rearrange("p (a b c d) -> p a b c d", a=shape[0], b=shape[1], c=shape[2])
                strides = []
                acc = esz
                for d_ in reversed(shape):
                    strides.insert(0, acc); acc *= d_
                return V(ap, "sb", at * 4, [128] + list(shape), [0] + strides, esz)
        A = Arena()

        def psv(b, ty=F32):
            if ty == F32:
                return V(banks[b][:, :], "ps", b * 2048, [128, 512], [0, 4], 4)
            return V(banks[b][:, :].bitcast(BF16), "ps", b * 2048, [128, 1024], [0, 2], 2)

        def dma(q, out, in_, reads, writes):
            sc.add(q, lambda e: e.dma_start(out=out, in_=in_), reads=reads, writes=writes, dma=True)

        def act(out, in_, func, scale=1.0, bias=0.0, accum=None, extra_r=()):
            r = [in_] + list(extra_r)
            sa = scale.ap if isinstance(scale, V) else scale
            ba = bias.ap if isinstance(bias, V) else bias
            if isinstance(scale, V): r.append(scale)
            if isinstance(bias, V): r.append(bias)
            w = [out] + ([accum] if accum is not None else [])
            kw = {}
            if accum is not None:
                kw["accum_out"] = accum.ap
            sc.add("act", lambda e: e.activation(out.ap, in_.ap, func, bias=ba, scale=sa, **kw), reads=r, writes=w)

        def tt(eng, out, a, b, op):
            sc.add(eng, lambda e: e.tensor_tensor(out.ap, a.ap, b.ap, op), reads=[a, b], writes=[out])

        def ts(eng, out, a, s1, s2, op0, op1=None):
            r = [a] + [x for x in (s1, s2) if isinstance(x, V)]
            s1a = s1.ap if isinstance(s1, V) else s1
            s2a = s2.ap if isinstance(s2, V) else s2
            if op1 is None:
                sc.add(eng, lambda e: e.tensor_scalar(out.ap, a.ap, s1a, None, op0), reads=r, writes=[out])
            else:
                sc.add(eng, lambda e: e.tensor_scalar(out.ap, a.ap, s1a, s2a, op0, op1), reads=r, writes=[out])

        def stt(eng, out, a, s, b, op0, op1, b_ap=None, b_reads=None):
            r = [a] + ([s] if isinstance(s, V) else []) + ([b] if b_reads is None else list(b_reads))
            sa = s.ap if isinstance(s, V) else s
            bap = b.ap if b_ap is None else b_ap
            sc.add(eng, lambda e: e.scalar_tensor_tensor(out.ap, a.ap, sa, bap, op0, op1), reads=r, writes=[out])

        def cp(eng, out, in_):
            if eng == "act":
                sc.add("act", lambda e: e.copy(out.ap, in_.ap), reads=[in_], writes=[out])
            else:
                sc.add(eng, lambda e: e.tensor_copy(out.ap, in_.ap), reads=[in_], writes=[out])

        def memset(eng, out, val):
            sc.add(eng, lambda e: e.memset(out.ap, val), writes=[out])

        def mm(out, pairs):
            r = []
            for l, rh in pairs:
                r += [l, rh]
            n = len(pairs)
            def em(e):
                ins = None
                for i, (l, rh) in enumerate(pairs):
                    ins = e.matmul(out.ap, l.ap, rh.ap, start=(i == 0), stop=(i == n - 1))
                return ins
            sc.add("pe", em, reads=r, writes=[out])

        def tr(out, in_, ident):
            sc.add("pe", lambda e: e.transpose(out.ap, in_.ap, ident.ap), reads=[in_, ident], writes=[out])

        def rsqrt(eng, y, x, t1):
            sc.add(eng, lambda e: e.tensor_scalar(y.ap.bitcast(I32), x.ap.bitcast(I32), -0.5, float(0x5f3759df),
                                                  ALU.mult, ALU.add), reads=[x], writes=[y])
            for _ in range(3):
                stt(eng, t1, y, -0.5, y, ALU.mult, ALU.mult)
                tt(eng, t1, t1, x, ALU.mult)
                stt(eng, y, t1, 1.5, y, ALU.add, ALU.mult)

        identb = A.alloc([128], BF16)
        identf = A.alloc([128])
        maskF = A.alloc([128]); maskB = A.alloc([128]); onesf = A.alloc([128])
        onesb = A.alloc([2], BF16)
        poolP = A.alloc([20, 128], BF16)
        par = A.alloc([PX])
        rowb = A.alloc([2048 + 512], BF16)
        sc_pp = A.alloc([8]); sc_pm = A.alloc([8]); hbm = A.alloc([16])
        xT = A.alloc([8, S], BF16)
        ybT = A.alloc([8, S], BF16)
        rstd = A.alloc([NT])
        g_ea = A.alloc([2, NT, 4]); g_ew = A.alloc([2, NT, 4]); g_cl = A.alloc([2, NT, 4]); g_dec = A.alloc([2, NT, 4])
        PH = A.top

        A.top = PH
        cst = A.alloc([24, 128])
        cvb32 = A.alloc([2048])
        dma("sp", cst.ap, cst_d.rearrange("p (a b) -> p a b", a=24), [], [cst])
        dma("sp", par.ap, par_d, [], [par])
        dma("sp", cvb32[0:1].ap, cvb_d, [], [cvb32])
        cp("dve", identb, cst[:, 0]); cp("dve", identf, cst[:, 0])
        cp("dve", maskF, cst[:, 1]); cp("dve", maskB, cst[:, 2]); cp("dve", onesf, cst[:, 3])
        memset("dve", onesb, 1.0)
        cp("act", poolP, cst[:, 4:24])
        cp("dve", rowb[0:1, 0:2048], cvb32[0:1])
        memset("dve", rowb[0:1, 2048:2560], 1.0)
        ts("dve", sc_pp, par[:, P_PSC:P_PSC + 8], 0.5, None, ALU.mult)
        ts("dve", sc_pm, par[:, P_MG:P_MG + 8], 0.25, None, ALU.mult)
        ts("dve", hbm, par[:, P_BM:P_BM + 16], 0.5, None, ALU.mult)

        NSLOT = 8
        stg = [A.alloc([1024]) for _ in range(NSLOT)]
        cbf = [A.alloc([1024], BF16) for _ in range(NSLOT)]
        it = [0]
        nstore = [0]

        class ConvPipe:
            def __init__(s, stg_, cbf_):
                s.stg, s.cbf, s.n, s.pending = stg_, cbf_, 0, []
            def load(s, item):
                src, c, col0, ncols, dsts, srcs_fn, scl, key, eng = item
                k = s.n % len(s.stg); s.n += 1
                st_, cb_ = s.stg[k][:, 0:ncols], s.cbf[k][:, 0:ncols]
                dma("sp", st_.ap, src[c * 128:(c + 1) * 128, col0:col0 + ncols], [], [st_])
                s.pending.append((st_, cb_, dsts, srcs_fn, scl, key, eng))
            def cast_store(s):
                if not s.pending:
                    return
                st_, cb_, dsts, srcs_fn, scl, key, eng = s.pending.pop(0)
                if eng == "act":
                    act(cb_, st_, AF.Identity, scale=scl)
                else:
                    ts("dve", cb_, st_, scl, None, ALU.mult)
                srcs = srcs_fn(cb_.ap)
                if not isinstance(dsts, list):
                    dsts, srcs = [dsts], [srcs]
                for dst, srcap in zip(dsts, srcs):
                    nstore[0] += 1
                    wk = [("wsc_up", nstore[0])] if key is None else [key]
                    sc.add("act" if key is None else "sp", lambda e, dst=dst, srcap=srcap: e.dma_start(out=dst, in_=srcap),
                           reads=[cb_], writes=wk, dma=True)

        gpre = lambda c: par[:, P_GPRE + c:P_GPRE + c + 1]
        def blocks_dest(first, stride, o0, n, c, jn=128):
            if stride == 1:
                v = wsc[first + o0:first + o0 + n]
            else:
                v = wsc[first:first + 8 * stride].rearrange("(o r) p e -> o r p e", r=stride)[o0:o0 + n, 0]
            return v[:, :, c * 128:c * 128 + jn].rearrange("o p j -> p o j")
        def srcn(n):
            return lambda ap: ap.rearrange("p (b j) -> p b j", b=n)
        def head_dsts(kind, c):
            return [wsc[B_HEAD + kind * 2 + h * 10:B_HEAD + kind * 2 + h * 10 + 2][:, :, c * 128:(c + 1) * 128]
                    .rearrange("o p j -> p o j") for h in range(4)]
        srch = lambda ap: [ap[:, h * 256:(h + 1) * 256].rearrange("p (b j) -> p b j", b=2) for h in range(4)]

        up = ConvPipe(stg, cbf)
        it = [0]
        def up_item(item):
            it[0] += 1
            up.load(item[:-1] + (("act", "dve")[it[0] % 2],))
            up.cast_store()
        for c in range(8):
            up_item((win_d, c, 7168, 16, blocks_dest(B_GATE, 1, 0, 1, c, 16), srcn(1), gpre(c), None, None))
        HK = ((K_Q, 2048), (K_K, 3072), (K_V, 4096), (K_O, 5120), (K_ZB, 6144))
        def head_item(h, kind, col, c, key, eng):
            dst = wsc[B_HEAD + kind * 2 + h * 10:B_HEAD + kind * 2 + h * 10 + 2][:, :, c * 128:(c + 1) * 128].rearrange("o p j -> p o j")
            return (win_d, c, col + h * 256, 256, dst, srcn(2), gpre(c), key, eng)
        for kind, col in HK:
            for c in range(8):
                up_item(head_item(0, kind, col, c, None, None))
        sc.add("sp", lambda e: None, reads=[("wsc_up", i_) for i_ in range(1, nstore[0] + 1)])

        deferred = []
        def defer(src, col0, first, stride, scale_fn, kind, nhalf=2, ncols=512, nblk=4):
            for c in range(8):
                for hh in range(nhalf):
                    deferred.append((src, c, col0 + hh * ncols, ncols, blocks_dest(first, stride, hh * nblk, nblk, c),
                                     srcn(nblk), scale_fn(c), ("wsc", kind), "dve"))
        for h_ in range(1, NH):
            for kind, col in HK:
                for c in range(8):
                    deferred.append(head_item(h_, kind, col, c, ("wsc", "head%d" % h_), "dve"))
        defer(win_d, 0, B_U, 1, gpre, "u")
        defer(wmix_d, 0, B_MIX, 1, lambda c: 1.0, "mix", nhalf=1, ncols=256, nblk=2)
        defer(wout_d, 0, B_OUT, 1, lambda c: 0.5, "out")
        defer(win_d, 1024, B_ZA, 1, gpre, "za")
        defer(wpp_d, 0, B_P3 + K_PP, 4, lambda c: sc_pp[:, c:c + 1], "pp")
        defer(win_d, 7184, B_P3 + K_GA, 4, gpre, "ga")
        defer(wpm_d, 0, B_P3 + K_PM, 4, lambda c: sc_pm[:, c:c + 1], "pm")
        defer(win_d, 8208, B_P3 + K_GBR, 4, gpre, "gb")
        dpipe = [None]
        def sprinkle(n=1):
            for _ in range(n):
                if dpipe[0] is None:
                    return
                if deferred:
                    dpipe[0].load(deferred.pop(0))
                if len(dpipe[0].pending) >= len(dpipe[0].stg) or not deferred:
                    dpipe[0].cast_store()
        def drain():
            while dpipe[0] is not None and (deferred or dpipe[0].pending):
                sprinkle()

        def wload(dst, b0, n, kinds=()):
            dma("sp", dst.ap.rearrange("p n c j -> p n (c j)"), wsc[b0:b0 + n].rearrange("n p e -> p n e"),
                [("wsc", k_) for k_ in kinds], [dst])

        import os
        STOP = int(os.environ.get('MK_STOP', '9'))
        for sq in range(NSEQ if STOP > 0 else 0):
            A.top = PH
            xn = [A.alloc([1024], BF16) for _ in range(3)]
            ss = A.alloc([NT]); rt1 = A.alloc([NT]); ssx = A.alloc([NT])
            wg = A.alloc([1, 8, 128], BF16)
            gsb = A.alloc([NT, 16])
            lfn = A.alloc([2, NT, 4]); gt1 = A.alloc([2, NT, 4]); gt2 = A.alloc([2, NT, 4])
            bnS = A.alloc([2, NT, 4]); tnS = A.alloc([2, NT, 4])
            A.top = max(A.top, PH + NT * 512 + 4096)
            xall = [A.alloc([1024]) for _ in range(NT)]
            dma("sp", wg[:, 0, :, 0:16].ap, wsc[B_GATE].rearrange("p (c j) -> p c j", c=8)[:, :, 0:16], [], [wg])
            for t in range(NT):
                dma("sp", xall[t].ap, x_d[sq, t * 128:(t + 1) * 128, :], [], [xall[t]])
                act(xn[t % 3], xall[t], AF.Square, accum=ss[:, t:t + 1])
            ts("dve", ssx, ss, 1.0 / D, EPS, ALU.mult, ALU.add)
            rsqrt("dve", rstd, ssx, rt1)
            for t in range(NT):
                k = t % 3
                ts("dve", xn[k], xall[t], rstd[:, t:t + 1], None, ALU.mult)
                pb = psv(t % 2, BF16)
                for c in range(8):
                    tr(pb[:, c * 128:(c + 1) * 128], xn[k][:, c * 128:(c + 1) * 128], identb)
                o_ = xT[:, :, t * 128:(t + 1) * 128]
                i_ = pb
                eng = "act" if t % 2 == 0 else "dve"
                if eng == "act":
                    sc.add("act", lambda e, o_=o_, i_=i_: e.copy(o_.ap, i_.ap.rearrange("p (c j) -> p c j", c=8)),
                           reads=[i_], writes=[o_])
                else:
                    sc.add("dve", lambda e, o_=o_, i_=i_: e.tensor_copy(o_.ap, i_.ap.rearrange("p (c j) -> p c j", c=8)),
                           reads=[i_], writes=[o_])
            if STOP == 1 and int(os.environ.get('MK_SUB', '9')) == 0:
                continue
            pg = psv(2)
            for t in range(NT):
                mm(pg[:, t * 16:(t + 1) * 16],
                   [(xT[:, c, t * 128:(t + 1) * 128], wg[:, 0, c, 0:16]) for c in range(8)])
            gbias = par[:, P_GB:P_GB + 16]
            gsf = gsb[:, :, 0:8]
            sc.add("dve", lambda e: e.tensor_tensor(gsf.ap, pg[:, 0:NT * 16].ap.rearrange("p (t k) -> p t k", k=16)[:, :, 0:8],
                                                    gbias[:, 0:8].ap.unsqueeze(1).broadcast_to([128, NT, 8]), ALU.add),
                   reads=[pg[:, 0:NT * 16], gbias], writes=[gsf])
            for t in range(NT):
                tt("dve", gsb[:, NT - 1 - t, 8:16], pg[:, t * 16 + 8:t * 16 + 16], gbias[:, 8:16], ALU.add)
            for dr in range(2):
                act(lfn[:, dr], gsb[:, :, dr * 8 + 4:dr * 8 + 8], AF.Exp, scale=-1.0)
            ts("dve", lfn, lfn, 1.0, None, ALU.add)
            act(lfn, lfn, AF.Ln)
            pc_ = psv(3)
            def flat(v_):
                return V(v_.ap.rearrange("p t h -> p (t h)"), v_.space, v_.off, [128, NT * 4], [0, 4], 4)
            for dr in range(2):
                rflat = flat(lfn[:, dr])
                mm(pc_[:, dr * 64:dr * 64 + NT * 4], [((maskF, maskB)[dr], rflat)])
                mm(pc_[:, 128 + dr * 64:128 + dr * 64 + NT * 4], [(onesf, rflat)])
            for dr in range(2):
                cp("dve", flat(bnS[:, dr]), pc_[:, dr * 64:dr * 64 + NT * 4])
                cp("dve", flat(tnS[:, dr]), pc_[:, 128 + dr * 64:128 + dr * 64 + NT * 4])
            for dr in range(2):
                ipre = gsb[:, :, dr * 8:dr * 8 + 4]
                o1 = gt1[:, dr]; o2 = gt2[:, dr]
                tt("dve", o1, ipre, bnS[:, dr], ALU.add)
                tt("dve", o2, o1, tnS[:, dr], ALU.subtract)
                act(g_ea[:, dr], o1, AF.Exp)
                act(g_ew[:, dr], o2, AF.Exp)
                act(g_cl[:, dr], bnS[:, dr], AF.Exp)
                ts("dve", g_cl[:, dr], g_cl[:, dr], 64.0, None, ALU.mult)
                act(g_dec[:, dr], tnS[:, dr], AF.Exp, scale=-1.0)

            if STOP == 1:
                continue
            A.top = PH
            wh = A.alloc([10, 8, 128], BF16)
            qkpre = A.alloc([4, S + 4], BF16)
            gate_all = V(arena[:, qkpre.off // 4:qkpre.off // 4 + NT * 256].rearrange("p (a b) -> p a b", a=NT),
                         "sb", qkpre.off, [128, NT, 256], [0, 1024, 4], 4)
            diag = A.alloc([20, 128], BF16)
            qT2 = A.alloc([2, S], BF16); kT2 = A.alloc([2, S], BF16)
            ktok = A.alloc([NT, 256], BF16); vtok = A.alloc([NT, 257], BF16)
            hfirst = A.alloc([NT, 256])
            Cst = [A.alloc([2, 2, 256]) for _ in range(2)]
            nst = [A.alloc([2, 2]) for _ in range(2)]
            Cb = [A.alloc([2, 2, 257], BF16) for _ in range(2)]
            Sp = [[A.alloc([128], BF16) for _ in range(4)] for _ in range(2)]
            ksc = [[A.alloc([256], BF16) for _ in range(2)] for _ in range(2)]
            tnh = [A.alloc([512]) for _ in range(2)]
            ga_ = [A.alloc([256])] * 2
            junk = [A.alloc([256], BF16) for _ in range(2)] * 2
            yb = [A.alloc([256], BF16) for _ in range(2)]
            ssh = A.alloc([NT]); sxh = A.alloc([NT]); rsh = A.alloc([NT]); t1h = A.alloc([NT])
            dn2 = [A.alloc([2, 1]) for _ in range(2)]
            ad2 = [A.alloc([2, 1]) for _ in range(2)]
            rd2 = [A.alloc([2, 1]) for _ in range(2)]
            memset("pool", vtok[:, :, 256:257], 1.0)
            if sq == 0:
                dpipe[0] = ConvPipe([A.alloc([512]) for _ in range(2)], [A.alloc([512], BF16) for _ in range(2)])
            build.dbg = dict(ybT=ybT, hfirst=hfirst, gate_all=gate_all, rsh=rsh, ssh=ssh, sxh=sxh, vtok=vtok, g_cl=g_cl, g_ea=g_ea, g_ew=g_ew, g_dec=g_dec, rd2=rd2[0], ad2=ad2[0], dn2=dn2[0], qT2=qT2, kT2=kT2, ktok=ktok)
            pcount = [0]
            HALF = NT // 2

            def flat2(v_, n, esz):
                return V(v_.ap.rearrange("p a b -> p (a b)"), v_.space, v_.off, [128, n], [0, esz], esz)

            for h in range(NH):
                if sq == 0 and h >= 1 and dpipe[0] is not None:
                    hk_ = ("wsc", "head%d" % h)
                    while any(it_[7] == hk_ for it_ in deferred) or any(p_[5] == hk_ for p_ in dpipe[0].pending):
                        sprinkle()
                wload(wh, B_HEAD + h * 10, 10, ["head%d" % h] if h >= 1 else [])
                memset("pool", qkpre[:, :, 0:2], 0.0)
                memset("pool", qkpre[:, :, S + 2:S + 4], 0.0)
                for ci in range(4):
                    cg = (0 if ci < 2 else 8) + h * 2 + (ci % 2)
                    for j in range(5):
                        ts("dve", diag[:, ci * 5 + j], identf, par[:, P_CW + cg * 5 + j:P_CW + cg * 5 + j + 1], None, ALU.mult)
                for ci in range(4):
                    for g in range(NG):
                        pb = psv(pcount[0] % 2); pcount[0] += 1
                        mm(pb, [(wh[:, ci, c, :], xT[:, c, g * 512:(g + 1) * 512]) for c in range(8)])
                        cp("act", qkpre[:, ci, 2 + g * 512:2 + (g + 1) * 512], pb)
                        if sq == 0:
                            sprinkle()
                for ci in range(4):
                    cg = (0 if ci < 2 else 8) + h * 2 + (ci % 2)
                    for g in range(NG):
                        pb = psv(2 + pcount[0] % 2); pcount[0] += 1
                        pairs = [(diag[:, ci * 5 + j], qkpre[:, ci, g * 512 + j:g * 512 + j + 512]) for j in range(5)]
                        pairs.append((rowb[0:1, cg * 128:(cg + 1) * 128], rowb[0:1, 2048:2560]))
                        mm(pb, pairs)
                        tb = tnh[pcount[0] % 2]
                        act(tb, pb, AF.Tanh, scale=0.5)
                        dst = (qT2 if ci < 2 else kT2)[:, ci % 2, g * 512:(g + 1) * 512]
                        stt("dve", dst, tb, 1.0, pb, ALU.add, ALU.mult)
                        if sq == 0:
                            sprinkle()
                for t in range(NT):
                    tc_ = slice(t * 128, (t + 1) * 128)
                    pb = psv(4 + t % 2)
                    mm(pb[:, 0:256], [(xT[:, c, tc_], wh[:, 4:6, c, :]) for c in range(8)])
                    cp("act", vtok[:, t, 0:256], pb[:, 0:256])
                    pk = psv(6, BF16)[:, (t % 4) * 256:(t % 4) * 256 + 256]
                    for j in range(2):
                        tr(pk[:, j * 128:(j + 1) * 128], kT2[:, j, tc_], identb)
                    cp("dve", ktok[:, t], pk)
                    pgate = psv(7 if t % 2 == 0 else 3)
                    mm(pgate, [(xT[:, c, tc_], wh[:, 6:10, c, :]) for c in range(8)])
                    tb = tnh[t % 2]
                    act(tb, pgate, AF.Tanh, scale=0.5)
                    a_ = ga_[t % 2]
                    stt("dve", a_, tb[:, 256:512], 1.0, pgate[:, 256:512], ALU.add, ALU.mult)
                    stt("dve", gate_all[:, t], tb[:, 0:256], 1.0, a_, ALU.add, ALU.mult)
                    if sq == 0:
                        sprinkle()
                memset("pool", Cst[1], 0.0); memset("pool", nst[1], 0.0); memset("pool", Cb[1], 0.0)
                psS = [psv(0)[:, 0:128], psv(0)[:, 128:256]]
                psP = [psv(2), psv(3)]
                psN = [psv(5)[:, 0:2], psv(5)[:, 64:66]]
                numbanks = [psv(6), psv(7), psv(1), psv(4)]
                NB_ = len(numbanks)
                def cflat(v_):
                    return V(v_.ap.rearrange("p a b -> p (a b)"), "sb", v_.off, [128, 512], [0, 4], 4)

                def tile_of(i, dr):
                    return i if dr == 0 else NT - 1 - i

                def stage_s(i):
                    for dr in range(2):
                        t = tile_of(i, dr)
                        tc_ = slice(t * 128, (t + 1) * 128)
                        mm(psS[dr], [(kT2[:, c, tc_], qT2[:, c, tc_]) for c in range(2)])
                    for dr in range(2):
                        sraw = yb[dr][:, (i % 2) * 128:(i % 2 + 1) * 128]
                        act(sraw, psS[dr], AF.Identity, scale=g_ea[:, dr, i, h:h + 1])
                        tt("pool", Sp[dr][i % 4], sraw, (maskF, maskB)[dr], ALU.mult)

                def stage_k(i):
                    for dr in range(2):
                        t = tile_of(i, dr)
                        act(ksc[dr][i % 2], ktok[:, t], AF.Identity, scale=g_ew[:, dr, i, h:h + 1])

                def stage_p(i):
                    for dr in range(2):
                        t = tile_of(i, dr)
                        ks = ksc[dr][i % 2]
                        for dc in range(2):
                            mm(psP[dr][:, dc * 256:(dc + 1) * 256], [(ks[:, dc * 128:(dc + 1) * 128], vtok[:, t, 0:256])])
                        for dc in range(2):
                            mm(psN[dr][:, dc:dc + 1], [(ks[:, dc * 128:(dc + 1) * 128], vtok[:, t, 256:257])])

                stage_k(0)
                for i0 in range(min(3, NT)):
                    stage_s(i0)
                stage_p(0)
                def den_evac(i):
                    k2 = i % 2
                    psNum = [numbanks[(2 * i) % NB_], numbanks[(2 * i + 1) % NB_]]
                    tt("dve", ad2[k2], dn2[k2], g_cl[:, :, i, h:h + 1], ALU.max)
                    sc.add("dve", lambda e, o=rd2[k2], a=ad2[k2]: e.reciprocal(o.ap, a.ap), reads=[ad2[k2]], writes=[rd2[k2]])
                    for dr in range(2):
                        t = tile_of(i, dr)
                        if i < HALF:
                            act(hfirst[:, t], psNum[dr][:, 0:256], AF.Identity, scale=rd2[k2][:, dr])
                        else:
                            htmp = tnh[k2][:, dr * 256:(dr + 1) * 256]
                            act(htmp, psNum[dr][:, 0:256], AF.Identity, scale=rd2[k2][:, dr])
                            tt("pool", hfirst[:, t], hfirst[:, t], htmp, ALU.add)
                            if i >= HALF + 1:
                                tp_ = tile_of(i - 1, dr)
                                jk_ = junk[(2 * i + dr) % 4]; hh_ = hfirst[:, tp_]; acc_ = ssh[:, tp_:tp_ + 1]
                                sc.add("dve", lambda e, jk_=jk_, hh_=hh_, acc_=acc_: e.scalar_tensor_tensor(
                                    jk_.ap, hh_.ap, 1.0, hh_.ap, ALU.mult, ALU.mult, accum_out=acc_.ap),
                                    reads=[hh_], writes=[jk_, acc_])

                for i in range(NT):
                    cbr = Cb[(i + 1) % 2]
                    cbw = Cb[i % 2]
                    psNum = [numbanks[(2 * i) % NB_], numbanks[(2 * i + 1) % NB_]]
                    if i + 1 < NT:
                        stage_k(i + 1)
                    if i >= 1:
                        den_evac(i - 1)
                    for dr in range(2):
                        t = tile_of(i, dr)
                        tc_ = slice(t * 128, (t + 1) * 128)
                        mm(psNum[dr][:, 0:257], [(Sp[dr][i % 4], vtok[:, t])] +
                           [(qT2[:, c, tc_], cbr[:, dr, c]) for c in range(2)])
                    for dr in range(2):
                        act(dn2[i % 2][:, dr], psNum[dr][:, 256:257], AF.Abs)
                    for dr in range(2):
                        stt("dve", cflat(Cst[i % 2][:, dr]), cflat(Cst[(i + 1) % 2][:, dr]), g_dec[:, dr, i, h:h + 1],
                            psP[dr], ALU.mult, ALU.add)
                        stt("dve", nst[i % 2][:, dr], nst[(i + 1) % 2][:, dr], g_dec[:, dr, i, h:h + 1],
                            psN[dr], ALU.mult, ALU.add)
                    cp("act", cbw[:, 0, :, 0:256], Cst[i % 2][:, 0])
                    cp("dve", cbw[:, 1, :, 0:256], Cst[i % 2][:, 1])
                    cp("dve", cbw[:, :, :, 256], nst[i % 2])
                    if sq == 0:
                        sprinkle()
                    if i + 3 < NT:
                        stage_s(i + 3)
                    if i + 1 < NT:
                        stage_p(i + 1)
                den_evac(NT - 1)
                for dr in range(2):
                    tp_ = tile_of(NT - 1, dr)
                    act(junk[dr], hfirst[:, tp_], AF.Square, accum=ssh[:, tp_:tp_ + 1])
                ts("dve", sxh, ssh, 1.0 / DH, EPS, ALU.mult, ALU.add)
                rsqrt("dve", rsh, sxh, t1h)
                for t in range(NT):
                    tc_ = slice(t * 128, (t + 1) * 128)
                    ybb = yb[t % 2]
                    stt("dve", ybb, hfirst[:, t], rsh[:, t:t + 1], gate_all[:, t], ALU.mult, ALU.mult)
                    pT = psv(t % 2, BF16)[:, 0:256]
                    for j in range(2):
                        tr(pT[:, j * 128:(j + 1) * 128], ybb[:, j * 128:(j + 1) * 128], identb)
                    o_ = ybT[:, h * 2:h * 2 + 2, tc_]
                    sc.add("act", lambda e, o_=o_, pT=pT: e.copy(o_.ap, pT.ap.rearrange("p (c j) -> p c j", c=2)),
                           reads=[pT], writes=[o_])

            if sq == 0:
                drain()
                dpipe[0] = None
            if STOP == 2:
                continue
            A.top = PH
            utok = A.alloc([NT, 1024], BF16)
            wu = A.alloc([8, 8, 128], BF16)
            pooledT = A.alloc([8, S], BF16)
            wmx = A.alloc([2, 8, 128], BF16)
            wout = A.alloc([8, 8, 128], BF16)
            TOP2 = A.top
            OVL = (NT * 512 + 4096) >= 11264
            if OVL:
                A.top = PH
            wz = [A.alloc([1, 8, 128], BF16) for _ in range(2)]
            w3 = [A.alloc([4, 8, 128], BF16) for _ in range(2)]
            yaT = A.alloc([8, 512], BF16)
            mgT = A.alloc([8, 512], BF16)
            tz = [A.alloc([512]) for _ in range(2)]
            sz = [A.alloc([512]) for _ in range(2)]
            xre1 = A.alloc([1024])
            if OVL:
                assert A.top <= PH + NT * 512 + 4096
                A.top = TOP2
            tA = [A.alloc([512]) for _ in range(2)]
            tB = [A.alloc([512]) for _ in range(2)]
            xre = [A.alloc([1024]), xre1]
            yo = [A.alloc([1024]) for _ in range(2)]
            ssA = [A.alloc([1]) for _ in range(2)]
            ssB = [A.alloc([1]) for _ in range(2)]
            sm3 = A.alloc([64])
            sm3i = [0]
            def smc3():
                k = sm3i[0]; sm3i[0] = (k + 1) % 64
                return sm3[:, k:k + 1]
            wload(wu, B_U, 8, ["u"]); wload(wmx, B_MIX, 2, ["mix"]); wload(wout, B_OUT, 8, ["out"])
            for t in range(NT):
                for hf in range(2):
                    pb = psv((t % 2) * 2 + hf)
                    mm(pb, [(xT[:, c, t * 128:(t + 1) * 128], wu[:, hf * 4:(hf + 1) * 4, c, :]) for c in range(8)])
                    cp("act" if hf == 0 else "dve", utok[:, t, hf * 512:(hf + 1) * 512], pb)
            pcn = [0]
            for g in range(NG):
                for cc in range(8):
                    pgp = cc // 2
                    pb = psv(4 + pcn[0] % 2); pcn[0] += 1
                    for tl in range(4):
                        t = g * 4 + tl
                        pairs = []
                        if t > 0:
                            pairs.append((utok[:, t - 1, cc * 128:(cc + 1) * 128], poolP[:, pgp * 5 + 0]))
                        var = 2
                        if t == 0:
                            var = 3
                        elif t == NT - 1:
                            var = 4
                        pairs.append((utok[:, t, cc * 128:(cc + 1) * 128], poolP[:, pgp * 5 + var]))
                        if t < NT - 1:
                            pairs.append((utok[:, t + 1, cc * 128:(cc + 1) * 128], poolP[:, pgp * 5 + 1]))
                        mm(pb[:, tl * 128:(tl + 1) * 128], pairs)
                    cp("act" if cc % 2 == 0 else "dve", pooledT[:, cc, g * 512:(g + 1) * 512], pb)
            def sweep1(g, ocs):
                gc = slice(g * 512, (g + 1) * 512)
                for oc in ocs:
                    k = oc % 2
                    w_ = wz[k]
                    wload(w_, B_ZA + oc, 1, ["za"])
                    pgp, oh = oc // 2, oc % 2
                    psM = psv(0 + k); psZ = psv(2 + k)
                    mm(psM, [(wmx[:, oh, pgp * 2 + kc, :], pooledT[:, pgp * 2 + kc, gc]) for kc in range(2)])
                    mm(psZ, [(w_[:, 0, c, :], xT[:, c, gc]) for c in range(8)])
                    act(tz[k], psZ, AF.Tanh, scale=0.5)
                    stt("dve", sz[k], tz[k], 1.0, psZ, ALU.add, ALU.mult)
                    tt("dve", yaT[:, oc], psM, sz[k], ALU.mult)

            def sweep2(g):
                gc = slice(g * 512, (g + 1) * 512)
                for oc in range(8):
                    k = oc % 2
                    w_ = w3[k]
                    wload(w_, B_P3 + oc * 4, 4, ["pp", "ga", "pm", "gb"])
                    psPA = psv(4 * k + 0); psGA = psv(4 * k + 1); psPB = psv(4 * k + 2); psGB = psv(4 * k + 3)
                    mm(psGA, [(w_[:, K_GA, c, :], xT[:, c, gc]) for c in range(8)])
                    mm(psGB, [(w_[:, K_GBR, c, :], xT[:, c, gc]) for c in range(8)])
                    mm(psPA, [(w_[:, K_PP, c, :], yaT[:, c, :]) for c in range(8)])
                    mm(psPB, [(w_[:, K_PM, c, :], ybT[:, c, gc]) for c in range(8)])
                    act(tA[k], psGA, AF.Tanh, scale=0.5, bias=hbm[:, oc:oc + 1])
                    act(tB[k], psGB, AF.Tanh, scale=0.5, bias=hbm[:, 8 + oc:8 + oc + 1])
                    stt("dve", tA[k], tA[k], 1.0, psPA, ALU.add, ALU.mult)
                    stt("dve", tB[k], tB[k], 1.0, psPB, ALU.add, ALU.mult)
                    tt("pool", mgT[:, oc], tA[k], tB[k], ALU.add)

            def out_tile(g, tl):
                t = g * 4 + tl
                k = tl % 2
                po = [psv(4 + 2 * k), psv(5 + 2 * k)]
                for hf in range(2):
                    mm(po[hf], [(mgT[:, c, tl * 128:(tl + 1) * 128], wout[:, hf * 4:(hf + 1) * 4, c, :]) for c in range(8)])
                dma("sp", xre[k].ap, x_d[sq, t * 128:(t + 1) * 128, :], [], [xre[k]])
                yob = V(yo[k].ap.bitcast(BF16), "sb", yo[k].off, [128, 2048], [0, 2], 2)
                act(yob[:, 0:512], po[0], AF.Square, scale=float(D ** -0.5), accum=ssA[k])
                act(yob[:, 512:1024], po[1], AF.Square, scale=float(D ** -0.5), accum=ssB[k])
                sx = smc3(); rs = smc3(); t1 = smc3()
                stt("dve", sx, ssA[k], EPS, ssB[k], ALU.add, ALU.add)
                rsqrt("dve", rs, sx, t1)
                for hf in range(2):
                    stt("dve", yo[k][:, hf * 512:(hf + 1) * 512], po[hf], rs,
                        par[:, P_GPOST + hf * 512:P_GPOST + (hf + 1) * 512], ALU.mult, ALU.mult)
                tt("pool", yo[k], yo[k], xre[k], ALU.add)
                dma("pool", y_d[sq, t * 128:(t + 1) * 128, :], yo[k].ap, [yo[k]], [("y", sq, t)])

            sweep1(0, range(8))
            for g in range(NG):
                sweep2(g)
                if g + 1 < NG:
                    out_tile(g, 0); out_tile(g, 1)
                    sweep1(g + 1, range(0, 4))
                    out_tile(g, 2); out_tile(g, 3)
                    sweep1(g + 1, range(4, 8))
                else:
                    for tl in range(4):
                        out_tile(g, tl)

        sc.add("pool", lambda e: None, reads=[("y", sq, t) for sq in range(NSEQ) for t in range(NT)])

        sc.finalize(csems, dsems)

        @block.sync
        def _(e):
            sc.emit("sp", e)

        @block.tensor
        def _(e):
            sc.emit("pe", e)

        @block.scalar
        def _(e):
            sc.emit("act", e)

        @block.vector
        def _(e):
            sc.emit("dve", e)

        @block.gpsimd
        def _(e):
            sc.emit("pool", e)

    return nc


_CACHE = {}


def _prep_shared(inp, S):
    return {
        "w_in": np.ascontiguousarray(np.asarray(inp["w_in"][0], np.float32)),
        "w_pp": np.ascontiguousarray(np.asarray(inp["w_proj_pool"][0], np.float32)),
        "w_pm": np.ascontiguousarray(np.asarray(inp["w_proj_mlstm"][0], np.float32)),
        "w_out": np.ascontiguousarray(np.asarray(inp["w_out"][0], np.float32)),
        "w_mix": np.ascontiguousarray(np.asarray(inp["pool_mix_w"][0], np.float32).reshape(1024, 256)),
        "consts": _consts(S),
        "params": _params(inp),
        "convb": np.ascontiguousarray(np.asarray(inp["conv_qk_b"][0], np.float32).reshape(1, 2048)),
    }


def run(inp, n_cores, nseq):
    x = np.asarray(inp["x"], np.float32)
    B, S, _ = x.shape
    assert B == n_cores * nseq
    key = (S, nseq)
    if key not in _CACHE:
        _CACHE[key] = build(S, nseq)
    nc = _CACHE[key]
    shared = _prep_shared(inp, S)
    in_maps = []
    for c in range(n_cores):
        m = dict(shared)
        m["x"] = np.ascontiguousarray(x[c * nseq:(c + 1) * nseq])
        in_maps.append(m)
    res = run_bass_kernel_spmd(nc, in_maps, core_ids=list(range(n_cores)))
    return np.concatenate([np.asarray(r["y"], np.float32) for r in res.results], axis=0)


def kernel(**inputs):
    return run(inputs, 8, 2)
```
